# Optimizing a Trainium2 kernel written in Bass

```python
import jax, jax.numpy as jnp
from jax import lax
import numpy as np

D_MODEL = 2048
BATCH = 2
SEQ = 8192
DEPTH = 1

CHUNK = 64
D_MIX = D_MODEL
D_SSM = D_MIX // 2
D_CONV = D_MIX - D_SSM
SSM_GROUP = 16
N_SSM_GROUPS = D_SSM // SSM_GROUP
SSM_STATE = 64
CONV_K = 31
D_FF = (((8 * D_MODEL) // 3 + 255) // 256) * 256
FFN_CONV_K = 3
EPS = 1e-6
DT_MIN = 1e-3
DT_MAX = 1e-1

kernel_name = "hybrid_s5_conformer_convffn_layer"


def rms_norm(x, gain):
    x32 = x.astype(jnp.float32)
    y = x32 * lax.rsqrt(jnp.mean(x32 * x32, axis=-1, keepdims=True) + EPS)
    return (y * gain.astype(jnp.float32)).astype(x.dtype)


def layer_norm(x, gain, bias):
    x32 = x.astype(jnp.float32)
    mu = jnp.mean(x32, axis=-1, keepdims=True)
    xc = x32 - mu
    y = xc * lax.rsqrt(jnp.mean(xc * xc, axis=-1, keepdims=True) + EPS)
    return (y * gain.astype(jnp.float32) + bias.astype(jnp.float32)).astype(x.dtype)


def causal_depthwise_conv(x, w):
    k, c = w.shape
    return lax.conv_general_dilated(
        x, w[:, None, :].astype(x.dtype), window_strides=(1,), padding=[(k - 1, 0)],
        dimension_numbers=("NWC", "WIO", "NWC"), feature_group_count=c)


def _linear_recurrence(e1, e2):
    a1, b1 = e1
    a2, b2 = e2
    return a1 * a2, a2 * b1 + b2


def s5_mixer(u, log_dt, lam_re, lam_im, b_re, b_im, c_re, c_im, d, w_glu):
    bsz, seq, _ = u.shape
    nc = seq // CHUNK
    u32 = u.astype(jnp.float32).reshape(bsz, seq, N_SSM_GROUPS, SSM_GROUP)
    lam = lax.complex(lam_re.astype(jnp.float32), lam_im.astype(jnp.float32))
    dt = jnp.exp(log_dt.astype(jnp.float32))[:, None]
    lam_dt = lam * dt
    lam_bar = jnp.exp(lam_dt)
    b = lax.complex(b_re.astype(jnp.float32), b_im.astype(jnp.float32))
    b_bar = ((lam_bar - 1.0) / lam)[..., None] * b
    c = lax.complex(c_re.astype(jnp.float32), c_im.astype(jnp.float32))
    bu = jnp.einsum("gph,blgh->blgp", b_bar, u32.astype(jnp.complex64))
    bu = bu.reshape(bsz, nc, CHUNK, N_SSM_GROUPS, SSM_STATE)
    a_local = jnp.broadcast_to(lam_bar, (1, 1, CHUNK, N_SSM_GROUPS, SSM_STATE))
    _, h_local = lax.associative_scan(_linear_recurrence, (a_local, bu), axis=2)
    steps = jnp.arange(1, CHUNK + 1, dtype=jnp.float32)[:, None, None]
    lam_pow = jnp.exp(lam_dt[None] * steps)
    a_chunk = jnp.broadcast_to(lam_pow[-1], (1, nc, N_SSM_GROUPS, SSM_STATE))
    _, s_end = lax.associative_scan(_linear_recurrence, (a_chunk, h_local[:, :, -1]), axis=1)
    s_prev = jnp.concatenate([jnp.zeros_like(s_end[:, :1]), s_end[:, :-1]], axis=1)
    h = h_local + lam_pow[None, None] * s_prev[:, :, None]
    h = h.reshape(bsz, seq, N_SSM_GROUPS, SSM_STATE)
    y = jnp.einsum("ghp,blgp->blgh", c, h).real + d.astype(jnp.float32).reshape(N_SSM_GROUPS, SSM_GROUP) * u32
    y = jax.nn.gelu(y.reshape(bsz, seq, D_SSM))
    y = y * jax.nn.sigmoid(y @ w_glu.astype(jnp.float32))
    return y.astype(u.dtype)


def conformer_conv_mixer(v, g, conv_w, ln_g, ln_b):
    z = v * jax.nn.sigmoid(g)
    z = causal_depthwise_conv(z, conv_w)
    z = layer_norm(z, ln_g, ln_b)
    return jax.nn.silu(z)


def conv_ffn(h, w_up, conv_w, w_down):
    up = causal_depthwise_conv(h @ w_up, conv_w)
    gate, val = jnp.split(up, 2, axis=-1)
    return (jax.nn.gelu(gate) * val) @ w_down


def setup_inputs(seed: int = 0) -> dict:
    key = jax.random.key(seed)
    ks = jax.random.split(key, 24)
    f32 = jnp.float32
    nrm = lambda k, shape, scale: jax.random.normal(k, shape, f32) * scale
    gain = lambda k, n: 1.0 + 0.01 * jax.random.normal(k, (DEPTH, n), f32)
    n_idx = jnp.arange(SSM_STATE, dtype=f32)
    lam_re = -0.5 + 0.01 * jax.random.normal(ks[3], (DEPTH, N_SSM_GROUPS, SSM_STATE), f32)
    lam_im = jnp.pi * n_idx + 0.01 * jax.random.normal(ks[4], (DEPTH, N_SSM_GROUPS, SSM_STATE), f32)
    log_dt = jax.random.uniform(ks[5], (DEPTH, N_SSM_GROUPS), f32, np.log(DT_MIN), np.log(DT_MAX))
    return {
        "x": nrm(ks[0], (BATCH, SEQ, D_MODEL), 1.0),
        "pre_mix_g": gain(ks[1], D_MODEL),
        "w_in": nrm(ks[2], (DEPTH, D_MODEL, D_SSM + 2 * D_CONV), D_MODEL ** -0.5),
        "ssm_log_dt": log_dt,
        "ssm_lam_re": lam_re,
        "ssm_lam_im": lam_im,
        "ssm_b_re": nrm(ks[6], (DEPTH, N_SSM_GROUPS, SSM_STATE, SSM_GROUP), (2 * SSM_GROUP) ** -0.5),
        "ssm_b_im": nrm(ks[7], (DEPTH, N_SSM_GROUPS, SSM_STATE, SSM_GROUP), (2 * SSM_GROUP) ** -0.5),
        "ssm_c_re": nrm(ks[8], (DEPTH, N_SSM_GROUPS, SSM_GROUP, SSM_STATE), (2 * SSM_STATE) ** -0.5),
        "ssm_c_im": nrm(ks[9], (DEPTH, N_SSM_GROUPS, SSM_GROUP, SSM_STATE), (2 * SSM_STATE) ** -0.5),
        "ssm_d": nrm(ks[10], (DEPTH, D_SSM), 1.0),
        "ssm_w_glu": nrm(ks[11], (DEPTH, D_SSM, D_SSM), D_SSM ** -0.5),
        "conv_w": nrm(ks[12], (DEPTH, CONV_K, D_CONV), CONV_K ** -0.5),
        "conv_ln_g": gain(ks[13], D_CONV),
        "conv_ln_b": nrm(ks[14], (DEPTH, D_CONV), 0.01),
        "w_out": nrm(ks[15], (DEPTH, D_MIX, D_MODEL), D_MIX ** -0.5),
        "post_mix_g": gain(ks[16], D_MODEL),
        "pre_ffn_g": gain(ks[17], D_MODEL),
        "ffn_w_up": nrm(ks[18], (DEPTH, D_MODEL, 2 * D_FF), D_MODEL ** -0.5),
        "ffn_conv_w": nrm(ks[19], (DEPTH, FFN_CONV_K, 2 * D_FF), FFN_CONV_K ** -0.5),
        "ffn_w_down": nrm(ks[20], (DEPTH, D_FF, D_MODEL), D_FF ** -0.5),
        "post_ffn_g": gain(ks[21], D_MODEL),
    }


def reference(x, pre_mix_g, w_in, ssm_log_dt, ssm_lam_re, ssm_lam_im, ssm_b_re, ssm_b_im,
              ssm_c_re, ssm_c_im, ssm_d, ssm_w_glu, conv_w, conv_ln_g, conv_ln_b, w_out,
              post_mix_g, pre_ffn_g, ffn_w_up, ffn_conv_w, ffn_w_down, post_ffn_g):
    for i in range(DEPTH):
        h = rms_norm(x, pre_mix_g[i])
        proj = h @ w_in[i]
        u = proj[..., :D_SSM]
        cv = proj[..., D_SSM:D_SSM + D_CONV]
        cg = proj[..., D_SSM + D_CONV:]
        y_ssm = s5_mixer(u, ssm_log_dt[i], ssm_lam_re[i], ssm_lam_im[i], ssm_b_re[i], ssm_b_im[i],
                         ssm_c_re[i], ssm_c_im[i], ssm_d[i], ssm_w_glu[i])
        y_conv = conformer_conv_mixer(cv, cg, conv_w[i], conv_ln_g[i], conv_ln_b[i])
        y = jnp.concatenate([y_ssm, y_conv], axis=-1) @ w_out[i]
        x = x + rms_norm(y, post_mix_g[i])
        h = rms_norm(x, pre_ffn_g[i])
        f = conv_ffn(h, ffn_w_up[i], ffn_conv_w[i], ffn_w_down[i])
        x = x + rms_norm(f, post_ffn_g[i])
    return x
```

```python
import contextlib
import os
import math
import numpy as np
import concourse.bass as bass
import concourse.mybir as mybir
from concourse.bass_utils import run_bass_kernel_spmd
from concourse.alu_op_type import AluOpType as ALU

F32 = mybir.dt.float32
BF16 = mybir.dt.bfloat16
AF = mybir.ActivationFunctionType

D = 2048
DS = 1024
G = 64
PSN = 64
DFF = 5632
KC = 31
EPS = 1e-6
NCORE = 8
SAME_ENGINE_SYNC = True


class StopBuild(Exception):
    pass


class Op:
    __slots__ = ("eng", "fn", "deps", "sig", "cnt", "dsem", "dval", "idx")


class Prog:
    def __init__(self, nc, es):
        self.nc = nc
        self.es = es
        self.ops = []
        self.res = {}
        self.dcount = {}
        self.dsems = {}
        self.esem = None
        self.cnt = None
        self.waited = None
        self.emitted = 0
        self.auto = 0

    def op(self, eng, fn, r=(), w=(), dma=None):
        o = Op()
        o.eng, o.fn, o.sig, o.cnt, o.dsem, o.dval = eng, fn, False, 0, dma, 0
        o.idx = len(self.ops)
        deps = {}
        for k in r:
            e = self.res.setdefault(k, [None, {}, []])
            if e[0] is not None:
                deps[id(e[0])] = e[0]
        for k in w:
            e = self.res.setdefault(k, [None, {}, []])
            if e[0] is not None:
                deps[id(e[0])] = e[0]
            for x in e[1].values():
                deps[id(x)] = x
            for x in e[2]:
                deps[id(x)] = x
        for k in r:
            e = self.res[k]
            if dma is None:
                e[1][eng] = o
            else:
                e[2].append(o)
        for k in w:
            self.res[k] = [o, {}, []]
        dl = []
        for d in deps.values():
            if d.dsem is None and d.eng == eng:
                if eng in ("pe", "sp") or not SAME_ENGINE_SYNC:
                    continue
            dl.append(d)
        o.deps = dl
        if dma is not None:
            self.dcount[dma] = self.dcount.get(dma, 0) + 16
            o.dval = self.dcount[dma]
        self.ops.append(o)
        return o

    def emit(self, final=True):
        nc = self.nc
        engs = {"pe": nc.tensor, "act": nc.scalar, "dve": nc.vector, "pool": nc.gpsimd, "sp": nc.sync}
        if self.esem is None:
            self.esem = {k: self.es.enter_context(nc.semaphore("es_" + k)) for k in engs}
            self.cnt = {k: 0 for k in engs}
            self.waited = {k: {} for k in engs}
        esem, cnt, waited = self.esem, self.cnt, self.waited
        for k in self.dcount:
            if k not in self.dsems:
                self.dsems[k] = self.es.enter_context(nc.semaphore("ds_" + k.replace("*", "g")))
        todo = self.ops[self.emitted:]
        self.emitted = len(self.ops)
        for o in todo:
            for d in o.deps:
                if d.dsem is None:
                    d.sig = True
        for o in todo:
            if o.dsem is None and o.sig:
                cnt[o.eng] += 1
                o.cnt = cnt[o.eng]
        for o in todo:
            e = engs[o.eng]
            need = {}
            for d in o.deps:
                if d.dsem is None:
                    key, val, sem = "e_" + d.eng, d.cnt, esem[d.eng]
                else:
                    dv = self.dcount[d.dsem] if d.dsem.endswith("*") else d.dval
                    key, val, sem = "d_" + d.dsem, dv, self.dsems[d.dsem]
                if need.get(key, (0, None))[0] < val:
                    need[key] = (val, sem)
            for key, (val, sem) in need.items():
                if waited[o.eng].get(key, 0) < val:
                    e.wait_ge(sem, val)
                    waited[o.eng][key] = val
            ins = o.fn(e)
            if o.dsem is not None:
                ins.then_inc(self.dsems[o.dsem], 16)
            elif o.sig:
                ins.then_inc(esem[o.eng], 1)
        if not final:
            return
        for k, v in self.dcount.items():
            if waited["sp"].get("d_" + k, 0) < v:
                nc.sync.wait_ge(self.dsems[k], v)


def build(NPRE, NMAIN, N=256, NMINI=32, debug=False):
    C = N // 8
    nc = bass.Bass("TRN2", target_bir_lowering=False)
    es = contextlib.ExitStack()
    P = Prog(nc, es)
    NTOK = NPRE * N + NMINI + NMAIN * N

    def din(name, shape, dt=F32):
        return nc.dram_tensor(name, list(shape), dt, kind="ExternalInput").ap()

    def dscr(name, shape, dt):
        return nc.dram_tensor(name, list(shape), dt, kind="Internal").ap()

    x_d = din("x", [NTOK, D])
    flag_d = din("flag", [128, 1])
    ident_d = din("ident", [128, 128])
    cmask_d = din("cmask", [128, 128])
    lvals_d = din("lvals", [64, 17])
    cvals_d = din("cvals", [64, C + 1])
    pre_mix_g = din("pre_mix_g", [D])
    w_in = din("w_in", [D, 3072])
    log_dt = din("ssm_log_dt", [G])
    lam_re = din("ssm_lam_re", [G, PSN])
    lam_im = din("ssm_lam_im", [G, PSN])
    b_re = din("ssm_b_re", [G, PSN, 16])
    b_im = din("ssm_b_im", [G, PSN, 16])
    c_re = din("ssm_c_re", [G, 16, PSN])
    c_im = din("ssm_c_im", [G, 16, PSN])
    ssm_d = din("ssm_d", [DS])
    w_glu = din("ssm_w_glu", [DS, DS])
    conv_w = din("conv_w", [KC, 1024])
    ln_g = din("conv_ln_g", [1024])
    ln_b = din("conv_ln_b", [1024])
    w_out = din("w_out", [D, D])
    post_mix_g = din("post_mix_g", [D])
    pre_ffn_g = din("pre_ffn_g", [D])
    w_up = din("ffn_w_up", [D, 2 * DFF])
    ffn_cw = din("ffn_conv_w", [3, 2 * DFF])
    w_down = din("ffn_w_down", [DFF, D])
    post_ffn_g = din("post_ffn_g", [D])
    out_d = nc.dram_tensor("out", [NMAIN * N, D], F32, kind="ExternalOutput").ap()

    win_b = dscr("win_b", [D, 3072], BF16)
    wglu_b = dscr("wglu_b", [DS, DS], BF16)
    wout_b = dscr("wout_b", [D, D], BF16)
    wup_b = dscr("wup_b", [D, 2 * DFF], BF16)
    wdn_b = dscr("wdn_b", [DFF, D], BF16)
    MI_d = dscr("MI_d", [G, 128, 128], BF16)
    MS_d = dscr("MS_d", [G, 128, 128], BF16)
    MO_d = dscr("MO_d", [G, 2, 64, 128], BF16)
    tab_d = dscr("tab_d", [3, 64, G, C + 1], F32)
    U_scr = dscr("U_scr", [8, G, 16, C], BF16)
    Y_scr = dscr("Y_scr", [8, G, 16, C], F32)
    dbg = {}

    def sb(name, shape, dt=F32):
        return es.enter_context(nc.sbuf_tensor("s_" + name, list(shape), dt))

    pes = contextlib.ExitStack()

    def sbp(name, shape, dt=F32):
        return pes.enter_context(nc.sbuf_tensor("t_" + name, list(shape), dt))

    def ps(name, shape, dt=F32):
        return es.enter_context(nc.psum_tensor("p_" + name, list(shape), dt))

    ident_f = sb("ident_f", [128, 128])
    ident_b = sb("ident_b", [128, 128], BF16)
    ones_f = sb("ones_f", [128, 128])
    ones_b = sb("ones_b", [128, 128], BF16)
    cmask = sb("cmask", [128, 128])
    flag = sb("flagt", [128, 1])
    gpre = sb("gpre", [128, 16])
    gpre2 = sb("gpre2", [128, 16])
    gpost = sb("gpost", [128, D])
    w31 = sb("w31", [128, 8, KC])
    lng = sb("lng", [128, 8])
    lnb = sb("lnb", [128, 8])
    w3 = sb("w3", [128, 88, 3])
    tcos = sb("tcos", [128, 32, C + 1])
    tsin = sb("tsin", [128, 32, C + 1])
    rtab = sb("rtab", [128, 32, 1])
    dmy = sb("dmy", [128, 8])
    dmy_d = dscr("dmy_d", [128, 8], F32)
    carry = sb("carry", [128, 2, 32])
    mmps = [ps("mm%d" % i, [128, 512]) for i in range(3)]
    trps = ps("trps", [128, 4, 256], BF16)
    Lre = ps("Lre", [128, 512])
    Lim = ps("Lim", [128, 512])
    Yps = [ps("Yps%d" % i, [128, 512]) for i in range(2)]
    mmi = [0]
    tri = [0]

    def mm_slot():
        i = mmi[0] % 3
        mmi[0] += 1
        return mmps[i], "mm%d" % i

    def tr_slot():
        i = tri[0] % 4
        tri[0] += 1
        return trps[:, i, :], "tr%d" % i

    def dma(eng, out, in_, r, w, sem, slow=False):
        def fn(e):
            if slow:
                return e.dma_start(out=out, in_=in_, allow_slow_non_contiguous=True)
            return e.dma_start(out=out, in_=in_)
        return P.op(eng, fn, r=r, w=w, dma=sem)

    def gdma(out, in_, key, slow=False):
        P.auto += 1
        o = dma("sp", out, in_, [], ["_g%d" % P.auto], "cg*", slow=slow)
        P.res[key] = [o, {}, []]
        return o

    def act_op(out, in_, func, r, w, scale=None, bias=None, accum=None):
        kw = {}
        if scale is not None:
            kw["scale"] = scale
        if bias is not None:
            kw["bias"] = bias
        if accum is not None:
            kw["accum_out"] = accum
        return P.op("act", lambda e: e.activation(out=out, in_=in_, func=func, **kw), r=r, w=w)

    def tt(eng, out, in0, in1, op, r, w):
        return P.op(eng, lambda e: e.tensor_tensor(out=out, in0=in0, in1=in1, op=op), r=r, w=w)

    def ts(eng, out, in0, s1, s2, op0, op1, r, w):
        if op1 is None:
            return P.op(eng, lambda e: e.tensor_scalar(out=out, in0=in0, scalar1=s1, scalar2=None, op0=op0), r=r, w=w)
        return P.op(eng, lambda e: e.tensor_scalar(out=out, in0=in0, scalar1=s1, scalar2=s2, op0=op0, op1=op1), r=r, w=w)

    def stt(out, in0, scalar, in1, op0, op1, r, w):
        return P.op("dve", lambda e: e.scalar_tensor_tensor(out=out, in0=in0, scalar=scalar, in1=in1, op0=op0, op1=op1), r=r, w=w)

    def mm(out, lhsT, rhs, start, stop, r, w):
        return P.op("pe", lambda e: e.matmul(out, lhsT=lhsT, rhs=rhs, start=start, stop=stop), r=r, w=w)

    def tr(out, in_, idn, r, w):
        return P.op("pe", lambda e: e.transpose(out, in_, idn), r=r, w=w)

    def cp(eng, out, in_, r, w):
        if eng == "act":
            return P.op("act", lambda e: e.copy(out=out, in_=in_), r=r, w=w)
        return P.op(eng, lambda e: e.tensor_copy(out=out, in_=in_), r=r, w=w)

    def split_cast(dst, src, nsplit, sem):
        rows = src.shape[0]
        step = rows // nsplit
        for i in range(nsplit):
            dma("pool", dst[i * step:(i + 1) * step, :], src[i * step:(i + 1) * step, :], [], [sem], sem)

    gdma(ident_f[:], ident_d, "ident_f")
    gdma(cmask[:], cmask_d, "cmask")
    gdma(flag[:], flag_d, "flag")
    gdma(gpre[:], pre_mix_g.rearrange("(k p) -> p k", p=128), "gpre", slow=True)
    gdma(gpre2[:], pre_ffn_g.rearrange("(k p) -> p k", p=128), "gpre2", slow=True)
    gdma(gpost[:], post_mix_g.partition_broadcast(128), "gpost")
    gdma(lng[:], ln_g.rearrange("(k p) -> p k", p=128), "lng", slow=True)
    gdma(lnb[:], ln_b.rearrange("(k p) -> p k", p=128), "lnb", slow=True)
    for k in range(8):
        gdma(w31[:, k, :], conv_w[:, k * 128:(k + 1) * 128].rearrange("t p -> p t"), "w31", slow=True)
    for t in range(3):
        for hh in range(4):
            gdma(w3[:, hh * 22:(hh + 1) * 22, t],
                 ffn_cw[t, hh * 2816:(hh + 1) * 2816].rearrange("(c p) -> p c", p=128), "w3", slow=True)
    cp("dve", ident_b[:], ident_f[:], ["ident_f"], ["ident_b"])
    P.op("pool", lambda e: e.memset(ones_f[:], 1.0), w=["ones_f"])
    P.op("pool", lambda e: e.memset(ones_b[:], 1.0), w=["ones_b"])
    P.op("pool", lambda e: e.memset(carry[:], 0.0), w=["carry"])
    P.op("pool", lambda e: e.memset(dmy[:], 0.0), w=["dmy"])
    split_cast(win_b, w_in, 4, "w_in")
    STOP = os.environ.get("KSTOP", "")
    if STOP == "p0":
        P.emit()
        es.close()
        return nc

    lamr = sbp("lamr", [64, G])
    lami = sbp("lami", [64, G])
    dtt = sbp("dtt", [64, G])
    lv = sbp("lv", [64, 17])
    cv = sbp("cv", [64, C + 1])
    gdma(lamr[:], lam_re.rearrange("g p -> p g"), "lamr", slow=True)
    gdma(lami[:], lam_im.rearrange("g p -> p g"), "lami", slow=True)
    gdma(dtt[:], log_dt.partition_broadcast(64), "dtt")
    gdma(lv[:], lvals_d, "lv")
    gdma(cv[:], cvals_d, "cv")
    bre = sbp("bre", [64, G, 16])
    bim = sbp("bim", [64, G, 16])
    gdma(bre[:], b_re.rearrange("g p h -> p g h"), "bre")
    gdma(bim[:], b_im.rearrange("g p h -> p g h"), "bim")
    cnat = sbp("cnat", [128, 2, 8, 64])
    gdma(cnat[:, 0], c_re.rearrange("(k g) h p -> (g h) k p", g=8), "cnat")
    gdma(cnat[:, 1], c_im.rearrange("(k g) h p -> (g h) k p", g=8), "cnat")
    dcol = sbp("dcol", [128, G])
    for i in range(8):
        gdma(dcol[16 * i:16 * i + 16, :], ssm_d.rearrange("(g h) -> h g", h=16), "dcol", slow=True)
    cT = sbp("cT", [64, 2, G, 16])
    for comp in range(2):
        for k in range(8):
            pt, pk = mm_slot()
            tr(pt[0:64, 0:128], cnat[:, comp, k, :], ident_f[:], ["cnat", "ident_f"], [pk])
            cp("act", cT[:, comp, 8 * k:8 * k + 8, :], pt[0:64, 0:128].rearrange("p (g h) -> p g h", h=16), [pk], ["cT"])
    act_op(dtt[:], dtt[:], AF.Exp, ["dtt"], ["dtt"])
    lrd = sbp("lrd", [64, G])
    lid = sbp("lid", [64, G])
    tt("dve", lrd[:], lamr[:], dtt[:], ALU.mult, ["lamr", "dtt"], ["lrd"])
    tt("dve", lid[:], lami[:], dtt[:], ALU.mult, ["lami", "dtt"], ["lid"])
    TWO_PI = 2.0 * math.pi
    MAGIC = 12582912.0

    def sincos(dst_c, dst_s, mag, ang, shape, key):
        t1 = sbp("sc1_" + key, shape)
        t2 = sbp("sc2_" + key, shape)
        for which, dst in ((0, dst_s), (1, dst_c)):
            src = ang
            if which == 1:
                ts("dve", t2[:], ang, math.pi / 2, None, ALU.add, None, [key + "ang"], [key + "t2"])
                src = t2[:]
            rk = [key + "ang", key + "t2"]
            ts("dve", t1[:], src, 1.0 / TWO_PI, MAGIC, ALU.mult, ALU.add, rk, [key + "t1"])
            ts("dve", t1[:], t1[:], -MAGIC, None, ALU.add, None, [key + "t1"], [key + "t1"])
            stt(t1[:], t1[:], -TWO_PI, src, ALU.mult, ALU.add, rk + [key + "t1"], [key + "t1"])
            ts("dve", t1[:], t1[:], math.pi, -math.pi, ALU.min, ALU.max, [key + "t1"], [key + "t1"])
            act_op(t1[:], t1[:], AF.Sin, [key + "t1"], [key + "t1"])
            tt("dve", dst, t1[:], mag, ALU.mult, [key + "t1", key + "mag"], [key + "dst%d" % which])

    ang = sbp("ang", [64, G, 17])
    mag = sbp("mag", [64, G, 17])
    apr = sbp("apr", [64, G, 17])
    api = sbp("api", [64, G, 17])
    lvb = lv[:, :].unsqueeze(1).broadcast_to((64, G, 17))
    tt("dve", ang[:], lid[:, :].unsqueeze(2).broadcast_to((64, G, 17)), lvb, ALU.mult, ["lid", "lv"], ["Aang"])
    tt("dve", mag[:], lrd[:, :].unsqueeze(2).broadcast_to((64, G, 17)), lvb, ALU.mult, ["lrd", "lv"], ["Amag0"])
    act_op(mag[:], mag[:], AF.Exp, ["Amag0"], ["Amag"])
    sincos(apr[:], api[:], mag[:], ang[:], [64, G, 17], "A")
    tang = sbp("tang", [64, G, C + 1])
    tone = sbp("tone", [64, G, C + 1])
    tco = sbp("tco", [64, G, C + 1])
    tsi = sbp("tsi", [64, G, C + 1])
    cvb = cv[:, :].unsqueeze(1).broadcast_to((64, G, C + 1))
    stt(tang[:], lid[:, :].unsqueeze(2).broadcast_to((64, G, C + 1)), 8.0, cvb, ALU.mult, ALU.mult, ["lid", "cv"], ["Tang"])
    P.op("pool", lambda e: e.memset(tone[:], 1.0), w=["Tmag"])
    sincos(tco[:], tsi[:], tone[:], tang[:], [64, G, C + 1], "T")
    dma("sp", tab_d[0], tco[:], ["Tdst1"], ["tab_d0"], "t0")
    dma("sp", tab_d[1], tsi[:], ["Tdst0"], ["tab_d1"], "t1")
    rt = sbp("rt", [64, G, C + 1])
    ts("dve", rt[:], lrd[:, :].unsqueeze(2).broadcast_to((64, G, C + 1)), 8.0, None, ALU.mult, None, ["lrd"], ["rt"])
    act_op(rt[:], rt[:], AF.Exp, ["rt"], ["rt"])
    dma("sp", tab_d[2], rt[:], ["rt"], ["tab_d2"], "t2")
    for comp, dst in ((0, tcos), (1, tsin), (2, rtab)):
        for par in range(2):
            cc_ = 1 if comp == 2 else C + 1
            dma("sp", dst[64 * par:64 * par + 64, :, :],
                tab_d[comp].rearrange("p (gp par) c -> par p gp c", par=2)[par][:, :, 0:cc_], ["tab_d%d" % comp], ["tabs%d%d" % (comp, par)], "tr%d%d" % (comp, par), slow=(comp == 2))
    TABS = ["tabs%d%d" % (c_, p_) for c_ in range(3) for p_ in range(2)]
    am1 = sbp("am1", [64, G])
    den = sbp("den", [64, G])
    wre = sbp("wre", [64, G])
    wim = sbp("wim", [64, G])
    t64a = sbp("t64a", [64, G])
    t64b = sbp("t64b", [64, G])
    a1r = apr[:, :, 9]
    a1i = api[:, :, 9]
    KA = ["Adst0", "Adst1"]
    ts("dve", am1[:], a1r, -1.0, None, ALU.add, None, KA, ["am1"])
    tt("dve", den[:], lamr[:], lamr[:], ALU.mult, ["lamr"], ["den"])
    tt("dve", t64a[:], lami[:], lami[:], ALU.mult, ["lami"], ["t64a"])
    tt("dve", den[:], den[:], t64a[:], ALU.add, ["den", "t64a"], ["den"])
    P.op("dve", lambda e: e.reciprocal(out=den[:], in_=den[:]), r=["den"], w=["den"])
    tt("dve", wre[:], am1[:], lamr[:], ALU.mult, ["am1", "lamr"], ["wre"])
    tt("dve", t64a[:], a1i, lami[:], ALU.mult, KA + ["lami"], ["t64a"])
    tt("dve", wre[:], wre[:], t64a[:], ALU.add, ["wre", "t64a"], ["wre"])
    tt("dve", wre[:], wre[:], den[:], ALU.mult, ["wre", "den"], ["wre"])
    tt("dve", wim[:], a1i, lamr[:], ALU.mult, KA + ["lamr"], ["wim"])
    tt("dve", t64b[:], am1[:], lami[:], ALU.mult, ["am1", "lami"], ["t64b"])
    tt("dve", wim[:], wim[:], t64b[:], ALU.subtract, ["wim", "t64b"], ["wim"])
    tt("dve", wim[:], wim[:], den[:], ALU.mult, ["wim", "den"], ["wim"])
    bbr = sbp("bbr", [64, G, 16])
    bbi = sbp("bbi", [64, G, 16])
    t16 = sbp("t16", [64, G, 16])
    wreb = wre[:, :].unsqueeze(2).broadcast_to((64, G, 16))
    wimb = wim[:, :].unsqueeze(2).broadcast_to((64, G, 16))
    tt("dve", bbr[:], bre[:], wreb, ALU.mult, ["bre", "wre"], ["bbr"])
    tt("dve", t16[:], bim[:], wimb, ALU.mult, ["bim", "wim"], ["t16"])
    tt("dve", bbr[:], bbr[:], t16[:], ALU.subtract, ["bbr", "t16"], ["bbr"])
    tt("dve", bbi[:], bim[:], wreb, ALU.mult, ["bim", "wre"], ["bbi"])
    tt("dve", t16[:], bre[:], wimb, ALU.mult, ["bre", "wim"], ["t16"])
    tt("dve", bbi[:], bbi[:], t16[:], ALU.add, ["bbi", "t16"], ["bbi"])

    GB = 8
    Hre = sbp("Hre", [64, GB, 8, 16])
    Him = sbp("Him", [64, GB, 8, 16])
    Gre = sbp("Gre", [64, GB, 8, 16])
    Gim = sbp("Gim", [64, GB, 8, 16])
    Ere = sbp("Ere", [64, GB, 8, 16])
    Ein = sbp("Ein", [64, GB, 8, 16])
    tq1 = sbp("tq1", [64, GB, 8, 16])
    MOst = sbp("MOst", [64, GB, 2, 128], BF16)
    MIst = sbp("MIst", [128, GB, 128], BF16)
    MSst = sbp("MSst", [128, GB, 128], BF16)
    tmi = sbp("tmi", [128, 128])
    SH = (64, GB, 8, 16)

    def cmul(dre, dim_, pr, pi, vr, vi, kd, neg_im=False):
        tt("dve", dre, pr, vr, ALU.mult, KA + ["bbr", "bbi", "cT"], [kd + "r"])
        tt("dve", tq1[:], pi, vi, ALU.mult, KA + ["bbr", "bbi", "cT"], ["tq1"])
        tt("dve", dre, dre, tq1[:], ALU.subtract, [kd + "r", "tq1"], [kd + "r"])
        tt("dve", dim_, pr, vi, ALU.mult, KA + ["bbr", "bbi", "cT"], [kd + "i"])
        tt("dve", tq1[:], pi, vr, ALU.mult, KA + ["bbr", "bbi", "cT"], ["tq1"])
        if neg_im:
            stt(dim_, dim_, -1.0, tq1[:], ALU.mult, ALU.subtract, [kd + "i", "tq1"], [kd + "i"])
        else:
            tt("dve", dim_, dim_, tq1[:], ALU.add, [kd + "i", "tq1"], [kd + "i"])

    for bt in range(G // GB):
        g0 = bt * GB
        gs = slice(g0, g0 + GB)
        def pw(t, lo, hi, rev):
            a = t[:, gs, lo:hi]
            if rev:
                a = t[:, gs, hi - 1:lo - 1 if lo > 0 else None:-1] if False else a
            return a
        for i in range(8):
            for (dr, di, idx) in ((Hre, Him, 7 - i), (Gre, Gim, 15 - i)):
                pr = apr[:, gs, idx:idx + 1].broadcast_to((64, GB, 16))
                pi = api[:, gs, idx:idx + 1].broadcast_to((64, GB, 16))
                kd = "H" if dr is Hre else "Gm"
                o_r, o_i = dr[:, :, i, :], di[:, :, i, :]
                tq = tq1[:, :, i, :]
                tt("dve", o_r, pr, bbr[:, gs, :], ALU.mult, KA + ["bbr"], [kd + "r"])
                tt("dve", tq, pi, bbi[:, gs, :], ALU.mult, KA + ["bbi"], ["tq1"])
                tt("dve", o_r, o_r, tq, ALU.subtract, [kd + "r", "tq1"], [kd + "r"])
                tt("dve", o_i, pr, bbi[:, gs, :], ALU.mult, KA + ["bbi"], [kd + "i"])
                tt("dve", tq, pi, bbr[:, gs, :], ALU.mult, KA + ["bbr"], ["tq1"])
                tt("dve", o_i, o_i, tq, ALU.add, [kd + "i", "tq1"], [kd + "i"])
        pr = apr[:, gs, 9:17].unsqueeze(3).broadcast_to(SH)
        pi = api[:, gs, 9:17].unsqueeze(3).broadcast_to(SH)
        cr = cT[:, 0, gs, :].unsqueeze(2).broadcast_to(SH)
        ci = cT[:, 1, gs, :].unsqueeze(2).broadcast_to(SH)
        cmul(Ere[:], Ein[:], pr, pi, cr, ci, "E", neg_im=True)
        cp("act", MOst[:, :, 0, :], Ere[:].rearrange("p g j h -> p g (j h)"), ["Er"], ["MOst"])
        cp("act", MOst[:, :, 1, :], Ein[:].rearrange("p g j h -> p g (j h)"), ["Ei"], ["MOst"])
        for gl in range(GB):
            g = g0 + gl
            pt, pk = mm_slot()
            mm(pt[:, 0:128], Hre[:, gl].rearrange("p i h -> p (i h)"), Ere[:, gl].rearrange("p j h -> p (j h)"),
               True, False, ["Hr", "Er"], [pk])
            mm(pt[:, 0:128], Him[:, gl].rearrange("p i h -> p (i h)"), Ein[:, gl].rearrange("p j h -> p (j h)"),
               False, True, ["Hi", "Ei"], [pk])
            tt("dve", tmi[:], pt[:, 0:128], cmask[:], ALU.mult, [pk, "cmask"], ["tmi"])
            stt(MIst[:, gl, :], ident_f[:], dcol[:, g:g + 1], tmi[:], ALU.mult, ALU.add, ["ident_f", "dcol", "tmi"], ["MIst"])
            pt2, pk2 = mm_slot()
            tr(pt2[:, 0:64], Gre[:, gl].rearrange("p i h -> p (i h)"), ident_f[0:64, 0:64], ["Gmr", "ident_f"], [pk2])
            tr(pt2[:, 64:128], Gim[:, gl].rearrange("p i h -> p (i h)"), ident_f[0:64, 0:64], ["Gmi", "ident_f"], [pk2])
            cp("act", MSst[:, gl, :], pt2[:, 0:128], [pk2], ["MSst"])
        dma("sp", MI_d[gs].rearrange("g k m -> k g m"), MIst[:], ["MIst"], ["MI_d"], "m0")
        dma("sp", MS_d[gs].rearrange("g k m -> k g m"), MSst[:], ["MSst"], ["MS_d"], "m1")
        dma("sp", MO_d[gs].rearrange("g c p m -> p g c m"), MOst[:], ["MOst"], ["MO_d"], "m2")

    if STOP == "prep":
        P.emit()
        pes.close()
        es.close()
        return nc
    split_cast(wglu_b, w_glu, 2, "w_glu")
    split_cast(wout_b, w_out, 4, "w_out")
    split_cast(wup_b, w_up, 8, "w_up")
    split_cast(wdn_b, w_down, 8, "w_down")


    prep_dmas = [o for o in P.ops if o.dsem is not None]
    jk = []
    for en in ("pe", "act", "dve", "pool", "sp"):
        if en == "pe":
            o = P.op("pe", lambda e: e.matmul(mmps[0][0:8, 0:8], lhsT=ident_f[0:8, 0:8], rhs=ident_f[0:8, 0:8], start=True, stop=True), r=["ident_f"], w=["bar_pe", "mm0"])
        elif en == "act":
            o = P.op("act", lambda e: e.copy(out=dmy[:, 0:1], in_=ident_f[:, 0:1]), r=["ident_f", "dmy"], w=["bar_act"])
        elif en == "dve":
            o = P.op("dve", lambda e: e.tensor_copy(out=dmy[:, 1:2], in_=ident_f[:, 0:1]), r=["ident_f", "dmy"], w=["bar_dve"])
        elif en == "pool":
            o = P.op("pool", lambda e: e.tensor_copy(out=dmy[:, 2:3], in_=ident_f[:, 0:1]), r=["ident_f", "dmy"], w=["bar_pool"])
        else:
            o = P.op("sp", lambda e: e.dma_start(out=dmy_d[:, 0:1], in_=dmy[:, 4:5], allow_slow_non_contiguous=True), r=["dmy"], w=["bar_sp"], dma="bar")
            o.deps = list(o.deps) + [d for d in prep_dmas if not d.dsem.endswith("*") or True]
        jk.append(o)
    bars = ["bar_pe", "bar_act", "bar_dve", "bar_pool", "bar_sp"]
    P.op("pe", lambda e: e.matmul(mmps[0][0:8, 8:16], lhsT=ident_f[0:8, 0:8], rhs=ident_f[0:8, 0:8], start=True, stop=True), r=bars + ["ident_f"], w=["bar2_pe", "mm0"])
    P.op("act", lambda e: e.copy(out=dmy[:, 5:6], in_=ident_f[:, 0:1]), r=bars + ["ident_f"], w=["bar2_act"])
    P.op("dve", lambda e: e.tensor_copy(out=dmy[:, 6:7], in_=ident_f[:, 0:1]), r=bars + ["ident_f"], w=["bar2_dve"])
    P.op("pool", lambda e: e.tensor_copy(out=dmy[:, 7:8], in_=ident_f[:, 0:1]), r=bars + ["ident_f"], w=["bar2_pool"])
    P.op("sp", lambda e: e.dma_start(out=dmy_d[:, 1:2], in_=dmy[:, 4:5], allow_slow_non_contiguous=True), r=bars + ["dmy"], w=["bar2_sp"], dma="bar2")
    P.emit(final=False)
    pes.close()

    x_tm = sb("x_tm", [128, N // 128 if N >= 128 else 1, D])
    f_tm = sb("f_tm", [128, N // 128 if N >= 128 else 1, D])
    xn_tm = sb("xn_tm", [128, 1, D], BF16)
    stat = sb("stat", [128, 16])
    h_fm = sb("h_fm", [128, 16, N], BF16)
    h2_fm = sb("h2_fm", [128, 16, N + 2], BF16)
    act = sb("act", [128, 44, N], BF16)
    z_ext = sb("z_ext", [128, 8, 32 + N], BF16)
    sig = sb("sig", [128, 2, N])
    rr = sb("rr", [128, 4, N])
    sqsb = sb("sqsb", [128, 2, N])
    hl = sb("hl", [128, 4, 2, N], BF16)
    lnst = sb("lnst", [128, 4, N])
    diag = sb("diag", [128, 8, 128], BF16)
    Dre = sb("Dre", [128, 16, C])
    Dim = sb("Dim", [128, 16, C])
    Xre = sb("Xre", [128, 16, C + 1])
    Xim = sb("Xim", [128, 16, C + 1])
    Sf = sb("Sf", [128, 2, 16, C + 1])
    Sb = sb("Sb", [128, 2, 32, C], BF16)
    tmpA = sb("tmpA", [128, 16, C + 1])
    tmpB = sb("tmpB", [128, 16, C + 1])
    Ysb = sb("Ysb", [128, G, C])
    convsb = Ysb[:, :, :].rearrange("p g c -> p (g c)").rearrange("p (k n) -> p k n", n=N)
    MIs = sb("MIs", [128, 2, 8, 128], BF16)
    MSs = sb("MSs", [128, 2, 8, 128], BF16)
    MOs = sb("MOs", [128, 2, 4, 2, 128], BF16)
    WB = 256
    wring = sb("wring", [128, 3, 16, WB], BF16)
    DBW = 128
    dring = sb("dring", [128, 2, 44, DBW], BF16)

    cat = act[:, 0:16, :]
    u_perm = act[:, 16:24, :]
    U_pk = act[:, 24:32, :].rearrange("p k (g c) -> p (k g) c", c=C)
    gy = act[:, 32:40, :]
    y_fm = f_tm[:, 0, :].rearrange("p (k n) -> p k n", n=N) if N * 8 <= D else None


    P.op("pool", lambda e: e.memset(z_ext[:], 0.0), w=["z_ext"])
    P.op("pool", lambda e: e.memset(h2_fm[:], 0.0), w=["h2_fm"])
    wslot = [0]
    dslot = [0]
    mslot = [0]

    def wload(src_ap, key, kchunks):
        s = wslot[0] % 3
        wslot[0] += 1
        rk = "wr%d" % s
        dma("sp", wring[:, s, 0:kchunks, :], src_ap.rearrange("(k p) n -> p k n", p=128), [key], [rk], rk)
        return wring[:, s], rk

    UPK = ["U_pk%d" % i_ for i_ in range(8)]

    def tile(tok0, NT, kind):
        CT = NT // 8
        TB = [(o, min(128, NT - o)) for o in range(0, NT, 128)]
        for bi, (o, sz) in enumerate(TB):
            dma("sp", x_tm[0:sz, bi, :], x_d[tok0 + o:tok0 + o + sz, :], [], ["x_tm%d" % bi], "xl%d" % bi)

        def norm_to_fm(src_tm, dst_fm, col0, gp, srckeys, kpre):
            for bi, (o, sz) in enumerate(TB):
                act_op(xn_tm[0:sz, 0, :], src_tm[0:sz, bi, :], AF.Square, [srckeys[bi]], ["xn0", "xnst%d" % bi],
                       accum=stat[0:sz, bi:bi + 1])
                ts("dve", stat[0:sz, 4 + bi:5 + bi], stat[0:sz, bi:bi + 1], 1.0 / D, EPS, ALU.mult, ALU.add,
                   ["xnst%d" % bi], ["st%d" % bi])
                act_op(stat[0:sz, 4 + bi:5 + bi], stat[0:sz, 4 + bi:5 + bi], AF.Sqrt, ["st%d" % bi], ["st%d" % bi])
                P.op("dve", lambda e, bi=bi, sz=sz: e.reciprocal(out=stat[0:sz, 8 + bi:9 + bi], in_=stat[0:sz, 4 + bi:5 + bi]),
                     r=["st%d" % bi], w=["rs%d" % bi])
                act_op(f_tm[0:sz, bi, :], src_tm[0:sz, bi, :], AF.Identity, [srckeys[bi], "rs%d" % bi],
                       ["f_tm%d" % bi] + (["y_fm%d" % k_ for k_ in range(8)] if bi == 0 else []),
                       scale=stat[0:sz, 8 + bi:9 + bi])
            if STOP == "A0":
                raise StopBuild()
            for kc in range(16):
                pt, pk = mm_slot()
                for bi, (o, sz) in enumerate(TB):
                    tr(pt[:, o:o + sz], f_tm[0:sz, bi, kc * 128:(kc + 1) * 128], ident_f[0:sz, 0:sz],
                       ["f_tm%d" % bi, "ident_f"], [pk])
                ts("dve", dst_fm[:, kc, col0:col0 + NT], pt[:, 0:NT], gp[:, kc:kc + 1], None, ALU.mult, None,
                   [pk, "gpre", "gpre2"], [kpre + "%d" % kc])

        norm_to_fm(x_tm, h_fm, 0, gpre, ["x_tm%d" % bi for bi in range(len(TB))], "h_fm")
        hk = ["h_fm%d" % kc for kc in range(16)]
        if STOP == "A":
            raise StopBuild()

        def proj_chunk(wt, cofs, rhs_fm, ncol, rkeys, kch=16):
            pt, pk = mm_slot()
            for k in range(kch):
                mm(pt[:, 0:ncol], wt[:, k, cofs:cofs + 128], rhs_fm(k), k == 0, k == kch - 1, rkeys(k), [pk])
            return pt, pk

        for blk in range(4):
            wt, wk = wload(win_b[:, blk * WB:(blk + 1) * WB], "w_in", 16)
            for c2 in range(2):
                m = blk * 2 + c2
                pt, pk = proj_chunk(wt, c2 * 128, lambda k: h_fm[:, k, 0:NT], NT, lambda k: [wk, hk[k]])
                cp("act", u_perm[:, m, 0:NT].rearrange("p (i c) -> p c i", i=8),
                   pt[:, 0:NT].rearrange("p (c i) -> p c i", i=8), [pk], ["act%d" % (16 + m)])
                dma("sp", U_scr[:, 8 * m:8 * m + 8, :, 0:CT].rearrange("i g h c -> (g h) i c"),
                    u_perm[:, m, 0:NT].rearrange("p (i c) -> p i c", i=8), ["act%d" % (16 + m)], ["U_scr%d" % m], "ub%d" % m)
        for i in range(8):
            dma("sp", U_pk[16 * i:16 * i + 16, :, 0:CT], U_scr[i, :, :, 0:CT].rearrange("g h c -> h g c"),
                ["U_scr%d" % m_ for m_ in range(8)], ["U_pk%d" % i] + (["act%d" % c_ for c_ in range(24, 32)] if i == 0 else []), "ur%d" % i)
        if STOP == "B":
            raise StopBuild()
        if kind != "prefix":
            for q in range(4):
                wv, wvk = wload(win_b[:, 1024 + q * WB:1024 + (q + 1) * WB], "w_in", 16)
                wg, wgk = wload(win_b[:, 2048 + q * WB:2048 + (q + 1) * WB], "w_in", 16)
                for c2 in range(2):
                    kk = q * 2 + c2
                    pg, pgk = proj_chunk(wg, c2 * 128, lambda k: h_fm[:, k, 0:NT], NT, lambda k: [wgk, hk[k]])
                    act_op(sig[:, kk % 2, 0:NT], pg[:, 0:NT], AF.Sigmoid, [pgk], ["sig%d" % (kk % 2)])
                    pv, pvk = proj_chunk(wv, c2 * 128, lambda k: h_fm[:, k, 0:NT], NT, lambda k: [wvk, hk[k]])
                    tt("dve", z_ext[:, kk, 32:32 + NT], pv[:, 0:NT], sig[:, kk % 2, 0:NT], ALU.mult,
                       [pvk, "sig%d" % (kk % 2)], ["z%d" % kk])

        for half in range(2):
            gb0 = half * 32
            for sub in range(4):
                s = mslot[0] % 2
                mslot[0] += 1
                gsub = slice(gb0 + sub * 8, gb0 + sub * 8 + 8)
                dma("sp", MSs[:, s], MS_d[gsub].rearrange("g k m -> k g m"), ["MS_d"], ["MSs%d" % s], "ms%d" % s)
                for gl in range(8):
                    g = gb0 + sub * 8 + gl
                    par, gpl = g % 2, (g - gb0) // 2
                    mm(Lre[64 * par:64 * par + 64, gpl * CT:(gpl + 1) * CT], MSs[:, s, gl, 0:64], U_pk[:, g, 0:CT],
                       True, True, ["MSs%d" % s, ] + UPK, ["Lre"])
                    mm(Lim[64 * par:64 * par + 64, gpl * CT:(gpl + 1) * CT], MSs[:, s, gl, 64:128], U_pk[:, g, 0:CT],
                       True, True, ["MSs%d" % s, ] + UPK, ["Lim"])
                if sub == 3 and STOP == "L":
                    raise StopBuild()
                if sub == 3:
                    gps = slice(half * 16, half * 16 + 16)
                    lre = Lre[:, 0:16 * CT].rearrange("p (g c) -> p g c", c=CT)
                    lim = Lim[:, 0:16 * CT].rearrange("p (g c) -> p g c", c=CT)
                    co = tcos[:, gps, 1:CT + 1]
                    si = tsin[:, gps, 1:CT + 1]
                    tA = tmpA[:, :, 0:CT]
                    tB = tmpB[:, :, 0:CT]
                    tt("dve", Dre[:, :, 0:CT], lre, co, ALU.mult, ["Lre", ] + TABS, ["Dre"])
                    tt("dve", tA, lim, si, ALU.mult, ["Lim", ] + TABS, ["tmpA"])
                    tt("dve", Dre[:, :, 0:CT], Dre[:, :, 0:CT], tA, ALU.add, ["Dre", "tmpA"], ["Dre"])
                    tt("dve", Dim[:, :, 0:CT], lim, co, ALU.mult, ["Lim", ] + TABS, ["Dim"])
                    tt("dve", tB, lre, si, ALU.mult, ["Lre", ] + TABS, ["tmpB"])
                    tt("dve", Dim[:, :, 0:CT], Dim[:, :, 0:CT], tB, ALU.subtract, ["Dim", "tmpB"], ["Dim"])
                    cp("dve", Xre[:, :, 0:1], carry[:, 0, gps].unsqueeze(2), ["carry"], ["Xre"])
                    cp("dve", Xim[:, :, 0:1], carry[:, 1, gps].unsqueeze(2), ["carry"], ["Xim"])
                    for gpl in range(16):
                        gp = half * 16 + gpl
                        for (X, Dd, kx, kd_) in ((Xre, Dre, "Xre", "Dre"), (Xim, Dim, "Xim", "Dim")):
                            P.op("dve", lambda e, X=X, Dd=Dd, gpl=gpl, gp=gp: e.tensor_tensor_scan(
                                out=X[:, gpl, 1:CT + 1], data0=rtab[:, gp, 0:1].broadcast_to((128, CT)), data1=Dd[:, gpl, 0:CT],
                                initial=X[:, gpl, 0:1], op0=ALU.mult, op1=ALU.add),
                                r=[kd_, *TABS, kx], w=[kx])
                    if STOP == "scan":
                        raise StopBuild()
                    co = tcos[:, gps, 0:CT + 1]
                    si = tsin[:, gps, 0:CT + 1]
                    tA = tmpA[:, :, 0:CT + 1]
                    tB = tmpB[:, :, 0:CT + 1]
                    sre = Sf[:, 0, :, 0:CT + 1]
                    sim = Sf[:, 1, :, 0:CT + 1]
                    xr = Xre[:, :, 0:CT + 1]
                    xi = Xim[:, :, 0:CT + 1]
                    tt("pool", sre, xr, co, ALU.mult, ["Xre", ] + TABS, ["Sre"])
                    tt("pool", tA, xi, si, ALU.mult, ["Xim", ] + TABS, ["tmpA"])
                    tt("pool", sre, sre, tA, ALU.subtract, ["Sre", "tmpA"], ["Sre"])
                    tt("pool", sim, xi, co, ALU.mult, ["Xim", ] + TABS, ["Sim"])
                    tt("pool", tB, xr, si, ALU.mult, ["Xre", ] + TABS, ["tmpB"])
                    tt("pool", sim, sim, tB, ALU.add, ["Sim", "tmpB"], ["Sim"])
                    cp("act", carry[:, 0, gps], Sf[:, 0, :, CT], ["Sre"], ["carry"])
                    cp("act", carry[:, 1, gps], Sf[:, 1, :, CT], ["Sim"], ["carry"])
                    if kind != "prefix":
                        cp("act", Sb[:, 0, gps, 0:CT], Sf[:, 0, :, 0:CT], ["Sre"], ["Sb"])
                        cp("act", Sb[:, 1, gps, 0:CT], Sf[:, 1, :, 0:CT], ["Sim"], ["Sb"])
            if kind == "prefix":
                continue
            for sub in range(4):
                s = mslot[0] % 2
                mslot[0] += 1
                gsub = slice(gb0 + sub * 8, gb0 + sub * 8 + 8)
                dma("sp", MIs[:, s], MI_d[gsub].rearrange("g k m -> k g m"), ["MI_d"], ["MIs%d" % s], "mi%d" % s)
                for par in range(2):
                    for comp in range(2):
                        dma("sp", MOs[64 * par:64 * par + 64, s, :, comp, :],
                            MO_d[gsub].rearrange("(gp par) c p m -> par c p gp m", par=2)[par, comp],
                            ["MO_d"], ["MOs%d_%d%d" % (s, par, comp)], "mo%d_%d%d" % (s, par, comp))
                for gl in range(8):
                    g = gb0 + sub * 8 + gl
                    par, gp = g % 2, g // 2
                    gi = g - gb0
                    yp = Yps[gi // 16]
                    yk = "Yps%d" % (gi // 16)
                    cs = slice((gi % 16) * CT, (gi % 16 + 1) * CT)
                    mm(yp[:, cs], MIs[:, s, gl, :], U_pk[:, g, 0:CT], True, False, ["MIs%d" % s, ] + UPK, [yk])
                    mm(yp[:, cs], MOs[64 * par:64 * par + 64, s, gl // 2, 0, :], Sb[64 * par:64 * par + 64, 0, gp, 0:CT],
                       False, False, ["MOs%d_%d%d" % (s, a_, b_) for a_ in range(2) for b_ in range(2)] + ["Sb"], [yk])
                    mm(yp[:, cs], MOs[64 * par:64 * par + 64, s, gl // 2, 1, :], Sb[64 * par:64 * par + 64, 1, gp, 0:CT],
                       False, True, ["MOs%d_%d%d" % (s, a_, b_) for a_ in range(2) for b_ in range(2)] + ["Sb"], [yk])
            for hh in range(2):
                cp("act", Ysb[:, gb0 + 16 * hh:gb0 + 16 * hh + 16, 0:CT],
                   Yps[hh][:, 0:16 * CT].rearrange("p (g c) -> p g c", c=CT), ["Yps%d" % hh], ["Ysb"] + ["cv%d" % k_ for k_ in range(8)])
        if kind == "prefix":
            return
        for j in range(8):
            dma("sp", Y_scr[j, :, :, 0:CT].rearrange("g h c -> h g c"), Ysb[16 * j:16 * j + 16, :, 0:CT],
                ["Ysb"], ["Y_scr%d" % j], "yb%d" % j)
        for k in range(8):
            dma("sp", y_fm[:, k, 0:NT].rearrange("p (j c) -> p j c", j=8),
                Y_scr[:, 8 * k:8 * k + 8, :, 0:CT].rearrange("j g h c -> (g h) j c"),
                ["Y_scr%d" % j_ for j_ in range(8)] + ([] if k == 0 else ["f_tm0"]),
                ["y_fm%d" % k] + (["f_tm0"] if k == 0 else []), "yr%d" % k)

        if kind == "main" and STOP == "m_ssm":
            raise StopBuild()
        dgi = [0]
        sp_, spk = Lre, "Lre"
        sq_, sqk = Lim, "Lim"
        for kk in range(8):
            if kind == "main" and STOP == "c1b" and kk == 1:
                raise StopBuild()
            if kind == "main" and STOP == "c1c" and kk == 2:
                raise StopBuild()
            pt, pk = mm_slot()
            for t in range(KC):
                ds_ = dgi[0] % 8
                dgi[0] += 1
                ts("pool", diag[:, ds_, :], ident_b[:], w31[:, kk, t:t + 1], None, ALU.mult, None,
                   ["ident_b", "w31"], ["dg%d" % ds_])
                tofs = 2 + t - ((t % 2) if os.environ.get("KEVEN") else 0)
                mm(pt[:, 0:NT], diag[:, ds_, :], z_ext[:, kk, tofs:tofs + NT], t == 0, t == KC - 1,
                   ["dg%d" % ds_, "z%d" % kk, "z_ext"], [pk])
            if kind == "main" and STOP == "c1":
                raise StopBuild()
            cp("act", convsb[:, kk, 0:NT], pt[:, 0:NT], [pk], ["cv%d" % kk, "Ysb"])
            if kind == "main" and STOP == "c1d":
                raise StopBuild()
            b2 = kk % 2
            act_op(sqsb[:, b2, 0:NT], pt[:, 0:NT], AF.Square, [pk], ["sq%d" % b2])
            cp("act", hl[:, 0, b2, 0:NT], pt[:, 0:NT], [pk], ["chi%d" % b2])
            tt("dve", hl[:, 1, b2, 0:NT], convsb[:, kk, 0:NT], hl[:, 0, b2, 0:NT], ALU.subtract,
               ["cv%d" % kk, "chi%d" % b2], ["clo%d" % b2])
            cp("act", hl[:, 2, b2, 0:NT], sqsb[:, b2, 0:NT], ["sq%d" % b2], ["shi%d" % b2])
            tt("dve", hl[:, 3, b2, 0:NT], sqsb[:, b2, 0:NT], hl[:, 2, b2, 0:NT], ALU.subtract,
               ["sq%d" % b2, "shi%d" % b2], ["slo%d" % b2])
            if kind == "main" and STOP == "c1f":
                raise StopBuild()
            mm(sp_[:, 0:NT], ones_b[:], hl[:, 0, b2, 0:NT], kk == 0, False, ["ones_b", "chi%d" % b2], [spk])
            mm(sp_[:, 0:NT], ones_b[:], hl[:, 1, b2, 0:NT], False, kk == 7, ["ones_b", "clo%d" % b2], [spk])
            mm(sq_[:, 0:NT], ones_b[:], hl[:, 2, b2, 0:NT], kk == 0, False, ["ones_b", "shi%d" % b2], [sqk])
            mm(sq_[:, 0:NT], ones_b[:], hl[:, 3, b2, 0:NT], False, kk == 7, ["ones_b", "slo%d" % b2], [sqk])
        if kind == "main" and STOP == "c2":
            raise StopBuild()
        mean = lnst[:, 0, 0:NT]
        var = lnst[:, 1, 0:NT]
        rstd = lnst[:, 2, 0:NT]
        act_op(mean, sp_[:, 0:NT], AF.Copy, [spk], ["ln_m"], scale=1.0 / 1024)
        tt("dve", var, mean, mean, ALU.mult, ["ln_m"], ["ln_v"])
        stt(var, sq_[:, 0:NT], 1.0 / 1024, var, ALU.mult, ALU.subtract, [sqk, "ln_v"], ["ln_v"])
        ts("dve", var, var, EPS, None, ALU.add, None, ["ln_v"], ["ln_v"])
        act_op(var, var, AF.Sqrt, ["ln_v"], ["ln_v"])
        P.op("dve", lambda e: e.reciprocal(out=rstd, in_=var), r=["ln_v"], w=["ln_r"])
        for kk in range(8):
            tt("dve", convsb[:, kk, 0:NT], convsb[:, kk, 0:NT], mean, ALU.subtract, ["cv%d" % kk, "ln_m"], ["cv%d" % kk])
            tt("dve", convsb[:, kk, 0:NT], convsb[:, kk, 0:NT], rstd, ALU.mult, ["cv%d" % kk, "ln_r"], ["cv%d" % kk])
            act_op(cat[:, 8 + kk, 0:NT], convsb[:, kk, 0:NT], AF.Silu, ["cv%d" % kk, "lng", "lnb"], ["act%d" % (8 + kk)],
                   scale=lng[:, kk:kk + 1], bias=lnb[:, kk:kk + 1])
        cp("pool", z_ext[:, :, 0:32], z_ext[:, :, NT:NT + 32], ["z%d" % kk for kk in range(8)] + ["z_ext"],
           ["z_ext"] + ["z%d" % kk for kk in range(8)])

        if kind == "main" and STOP == "m_conv":
            raise StopBuild()
        for k in range(8):
            act_op(gy[:, k, 0:NT], y_fm[:, k, 0:NT], AF.Gelu_apprx_tanh, ["y_fm%d" % k], ["act%d" % (32 + k)])
        for blk in range(4):
            wt, wk = wload(wglu_b[:, blk * WB:(blk + 1) * WB], "w_glu", 8)
            for c2 in range(2):
                m = blk * 2 + c2
                pt, pk = proj_chunk(wt, c2 * 128, lambda k: gy[:, k, 0:NT], NT, lambda k: [wk, "act%d" % (32 + k)], kch=8)
                act_op(sig[:, m % 2, 0:NT], pt[:, 0:NT], AF.Sigmoid, [pk], ["sig%d" % (m % 2)])
                act_op(rr[:, m % 2, 0:NT], y_fm[:, m, 0:NT], AF.Gelu_apprx_tanh, ["y_fm%d" % m], ["rr%d" % (m % 2)])
                tt("dve", cat[:, m, 0:NT].rearrange("p (c j) -> p j c", j=8),
                   rr[:, m % 2, 0:NT].rearrange("p (j c) -> p j c", j=8),
                   sig[:, m % 2, 0:NT].rearrange("p (j c) -> p j c", j=8), ALU.mult,
                   ["rr%d" % (m % 2), "sig%d" % (m % 2)], ["act%d" % m])

        if kind == "main" and STOP == "m_glu":
            raise StopBuild()
        for nb in range(D // WB):
            wt, wk = wload(wout_b[:, nb * WB:(nb + 1) * WB], "w_out", 16)
            for bi, (o, sz) in enumerate(TB):
                pt, pk = mm_slot()
                for k in range(16):
                    mm(pt[0:sz, 0:WB], cat[:, k, o:o + sz], wt[:, k, :], k == 0, k == 15, ["act%d" % k, wk], [pk])
                cp("act", f_tm[0:sz, bi, nb * WB:(nb + 1) * WB], pt[0:sz, 0:WB], [pk], ["f_tm%d" % bi] + (["y_fm%d" % k_ for k_ in range(8)] if bi == 0 else []))

        def post_norm_add(gtile, gkey):
            for bi, (o, sz) in enumerate(TB):
                act_op(xn_tm[0:sz, 0, :], f_tm[0:sz, bi, :], AF.Square, ["f_tm%d" % bi], ["xn0", "xnst%d" % bi],
                       accum=stat[0:sz, bi:bi + 1])
                ts("dve", stat[0:sz, 4 + bi:5 + bi], stat[0:sz, bi:bi + 1], 1.0 / D, EPS, ALU.mult, ALU.add,
                   ["xnst%d" % bi], ["st%d" % bi])
                act_op(stat[0:sz, 4 + bi:5 + bi], stat[0:sz, 4 + bi:5 + bi], AF.Sqrt, ["st%d" % bi], ["st%d" % bi])
                P.op("dve", lambda e, bi=bi, sz=sz: e.reciprocal(out=stat[0:sz, 8 + bi:9 + bi], in_=stat[0:sz, 4 + bi:5 + bi]),
                     r=["st%d" % bi], w=["rs%d" % bi])
                stt(f_tm[0:sz, bi, :], f_tm[0:sz, bi, :], stat[0:sz, 8 + bi:9 + bi], gtile[0:sz, :], ALU.mult, ALU.mult,
                    ["f_tm%d" % bi, "rs%d" % bi, gkey], ["f_tm%d" % bi])
                tt("dve", x_tm[0:sz, bi, :], x_tm[0:sz, bi, :], f_tm[0:sz, bi, :], ALU.add,
                   ["x_tm%d" % bi, "f_tm%d" % bi], ["x_tm%d" % bi])

        post_norm_add(gpost, "gpost")
        if kind == "main":
            dma("sp", gpost[:], post_ffn_g.partition_broadcast(128), [], ["gpost"], "gp")
        if kind == "main":
            pass
        norm_to_fm(x_tm, h2_fm, 2, gpre2, ["x_tm%d" % bi for bi in range(len(TB))], "h2_fm")
        h2k = ["h2_fm%d" % kc for kc in range(16)]
        if kind == "mini":
            ts("dve", h2_fm[:, :, 0:2], h2_fm[:, :, NT:NT + 2], flag[:, 0:1], None, ALU.mult, None,
               h2k + ["flag"], ["h2_fm", "h2halo"])
            return
        if kind == "main" and STOP == "m_wout":
            raise StopBuild()
        for q in range(22):
            wg, wgk = wload(wup_b[:, q * WB:(q + 1) * WB], "w_up", 16)
            wv, wvk = wload(wup_b[:, DFF + q * WB:DFF + (q + 1) * WB], "w_up", 16)
            for c2 in range(2):
                ch = q * 2 + c2
                res = []
                for (wt, wk, cidx, slot) in ((wg, wgk, ch, 0), (wv, wvk, 44 + ch, 1)):
                    pt, pk = proj_chunk(wt, c2 * 128, lambda k: h2_fm[:, k, 0:NT + 2], NT + 2,
                                        lambda k: [wk, h2k[k], "h2halo", "h2_fm"])
                    r_ = rr[:, 2 * (ch % 2) + slot, 0:NT]
                    rk_ = "rr%d" % (2 * (ch % 2) + slot)
                    act_op(r_, pt[:, 0:NT], AF.Identity, [pk, "w3"], [rk_], scale=w3[:, cidx, 0:1])
                    stt(r_, pt[:, 1:NT + 1], w3[:, cidx, 1:2], r_, ALU.mult, ALU.add, [pk, "w3", rk_], [rk_])
                    stt(r_, pt[:, 2:NT + 2], w3[:, cidx, 2:3], r_, ALU.mult, ALU.add, [pk, "w3", rk_], [rk_])
                    res.append((r_, rk_))
                (rg, rgk), (rv, rvk) = res
                act_op(rg, rg, AF.Gelu_apprx_tanh, [rgk], [rgk])
                tt("dve", act[:, ch, 0:NT], rg, rv, ALU.mult, [rgk, rvk], ["act%d" % ch])
        cp("pool", h2_fm[:, :, 0:2], h2_fm[:, :, NT:NT + 2], h2k + ["h2_fm"], ["h2_fm", "h2halo"] + h2k)
        if kind == "main" and STOP == "m_up":
            raise StopBuild()
        for nb in range(D // DBW):
            s = dslot[0] % 2
            dslot[0] += 1
            dk = "dr%d" % s
            dma("sp", dring[:, s], wdn_b[:, nb * DBW:(nb + 1) * DBW].rearrange("(k p) n -> p k n", p=128),
                ["w_down"], [dk], dk)
            for bi, (o, sz) in enumerate(TB):
                pt, pk = mm_slot()
                for k in range(44):
                    mm(pt[0:sz, 0:DBW], act[:, k, o:o + sz], dring[:, s, k, :], k == 0, k == 43,
                       ["act%d" % k, dk], [pk])
                cp("act", f_tm[0:sz, bi, nb * DBW:(nb + 1) * DBW], pt[0:sz, 0:DBW], [pk], ["f_tm%d" % bi])
        post_norm_add(gpost, "gpost")
        dma("sp", gpost[:], post_mix_g.partition_broadcast(128), [], ["gpost"], "gp")
        orow = tok0 - (NPRE * N + NMINI)
        for bi, (o, sz) in enumerate(TB):
            dma("sp", out_d[orow + o:orow + o + sz, :], x_tm[0:sz, bi, :], ["x_tm%d" % bi], ["out"], "os")

    t0 = 0
    try:
        for i in range(NPRE):
            if STOP == "bar":
                break
            tile(t0, N, "prefix")
            t0 += N
    except StopBuild:
        P.emit()
        es.close()
        return nc
    if STOP in ("bar", "prefix"):
        P.emit()
        es.close()
        return nc
    tile(t0, NMINI, "mini")
    if STOP == "mini":
        P.emit()
        es.close()
        return nc
    t0 += NMINI
    try:
        for i in range(NMAIN):
            tile(t0, N, "main")
            t0 += N
    except StopBuild:
        P.emit()
        es.close()
        return nc
    P.emit()
    es.close()
    return nc


def host_consts(C):
    ident = np.eye(128, dtype=np.float32)
    idx = np.arange(128) // 16
    cmask = (idx[:, None] <= idx[None, :]).astype(np.float32)
    lvals = np.tile(np.arange(-8, 9, dtype=np.float32)[None, :], (64, 1))
    cvals = np.tile(np.arange(0, C + 1, dtype=np.float32)[None, :], (64, 1))
    return ident, cmask, lvals, cvals


def run(inputs, NPRE, NMAIN, N=256, NMINI=32, trace=False):
    x = np.asarray(inputs["x"], dtype=np.float32)
    B, S, _ = x.shape
    seg = NMAIN * N
    nseg = S // seg
    assert B * nseg == NCORE
    nc = build(NPRE, NMAIN, N, NMINI)
    ident, cmask, lvals, cvals = host_consts(N // 8)
    shared = {"ident": ident, "cmask": cmask, "lvals": lvals, "cvals": cvals}
    for k, v in inputs.items():
        if k == "x":
            continue
        a = np.asarray(v, dtype=np.float32)
        shared[k] = np.ascontiguousarray(a.reshape(a.shape[1:]))
    pre = NPRE * N + NMINI
    in_maps = []
    for c in range(NCORE):
        b, sg = divmod(c, nseg)
        s0 = sg * seg
        xs = np.zeros((pre + seg, D), np.float32)
        lo = s0 - pre
        src_lo = max(lo, 0)
        xs[src_lo - lo:, :] = x[b, src_lo:s0 + seg, :]
        m = dict(shared)
        m["x"] = xs
        m["flag"] = np.full((128, 1), 1.0 if s0 > 0 else 0.0, np.float32)
        in_maps.append(m)
    res = run_bass_kernel_spmd(nc, in_maps, core_ids=list(range(NCORE)), trace=trace)
    out = np.zeros((B, S, D), np.float32)
    for c in range(NCORE):
        b, sg = divmod(c, nseg)
        out[b, sg * seg:(sg + 1) * seg, :] = res.results[c]["out"]
    return out, res


def kernel(**inputs):
    out, _ = run(inputs, NPRE=24, NMAIN=8)
    return out
```

```python
import contextlib
import os
import math
import numpy as np
import concourse.bass as bass
import concourse.mybir as mybir
from concourse.bass_utils import run_bass_kernel_spmd
from concourse.alu_op_type import AluOpType as ALU

F32 = mybir.dt.float32
BF16 = mybir.dt.bfloat16
AF = mybir.ActivationFunctionType

D = 2048
DS = 1024
G = 64
PSN = 64
DFF = 5632
KC = 31
EPS = 1e-6
NCORE = 8
SAME_ENGINE_SYNC = os.environ.get("KSES", "1") == "1"


class StopBuild(Exception):
    pass


class Op:
    __slots__ = ("eng", "fn", "deps", "sig", "cnt", "dsem", "dval", "idx")


class Prog:
    def __init__(self, nc, es):
        self.nc = nc
        self.es = es
        self.ops = []
        self.res = {}
        self.dcount = {}
        self.dsems = {}
        self.esem = None
        self.cnt = None
        self.waited = None
        self.emitted = 0
        self.auto = 0

    def op(self, eng, fn, r=(), w=(), dma=None):
        o = Op()
        o.eng, o.fn, o.sig, o.cnt, o.dsem, o.dval = eng, fn, False, 0, dma, 0
        o.idx = len(self.ops)
        deps = {}
        for k in r:
            e = self.res.setdefault(k, [None, {}, []])
            if e[0] is not None:
                deps[id(e[0])] = e[0]
        for k in w:
            e = self.res.setdefault(k, [None, {}, []])
            if e[0] is not None:
                deps[id(e[0])] = e[0]
            for x in e[1].values():
                deps[id(x)] = x
            for x in e[2]:
                deps[id(x)] = x
        for k in r:
            e = self.res[k]
            if dma is None:
                e[1][eng] = o
            else:
                e[2].append(o)
        for k in w:
            self.res[k] = [o, {}, []]
        dl = []
        for d in deps.values():
            if d.dsem is None and d.eng == eng:
                if eng in ("pe", "sp") or not SAME_ENGINE_SYNC:
                    continue
            dl.append(d)
        o.deps = dl
        if dma is not None:
            self.dcount[dma] = self.dcount.get(dma, 0) + 16
            o.dval = self.dcount[dma]
        self.ops.append(o)
        return o

    def emit(self, final=True):
        nc = self.nc
        engs = {"pe": nc.tensor, "act": nc.scalar, "dve": nc.vector, "pool": nc.gpsimd, "sp": nc.sync}
        if self.esem is None:
            self.esem = {k: self.es.enter_context(nc.semaphore("es_" + k)) for k in engs}
            self.cnt = {k: 0 for k in engs}
            self.waited = {k: {} for k in engs}
        esem, cnt, waited = self.esem, self.cnt, self.waited
        for k in self.dcount:
            if k not in self.dsems:
                self.dsems[k] = self.es.enter_context(nc.semaphore("ds_" + k.replace("*", "g")))
        todo = self.ops[self.emitted:]
        self.emitted = len(self.ops)
        for o in todo:
            for d in o.deps:
                if d.dsem is None:
                    d.sig = True
        for o in todo:
            if o.dsem is None and o.sig:
                cnt[o.eng] += 1
                o.cnt = cnt[o.eng]
        for o in todo:
            e = engs[o.eng]
            need = {}
            for d in o.deps:
                if d.dsem is None:
                    key, val, sem = "e_" + d.eng, d.cnt, esem[d.eng]
                else:
                    dv = self.dcount[d.dsem] if d.dsem.endswith("*") else d.dval
                    key, val, sem = "d_" + d.dsem, dv, self.dsems[d.dsem]
                if need.get(key, (0, None))[0] < val:
                    need[key] = (val, sem)
            for key, (val, sem) in need.items():
                if waited[o.eng].get(key, 0) < val:
                    e.wait_ge(sem, val)
                    waited[o.eng][key] = val
            ins = o.fn(e)
            if o.dsem is not None:
                ins.then_inc(self.dsems[o.dsem], 16)
            elif o.sig:
                ins.then_inc(esem[o.eng], 1)
        if not final:
            return
        for k, v in self.dcount.items():
            if waited["sp"].get("d_" + k, 0) < v:
                nc.sync.wait_ge(self.dsems[k], v)


def build(NPRE, NMAIN, N=256, NMINI=32, debug=False):
    C = N // 8
    nc = bass.Bass("TRN2", target_bir_lowering=False)
    es = contextlib.ExitStack()
    P = Prog(nc, es)
    NTOK = NPRE * N + NMINI + NMAIN * N

    def din(name, shape, dt=F32):
        return nc.dram_tensor(name, list(shape), dt, kind="ExternalInput").ap()

    def dscr(name, shape, dt):
        return nc.dram_tensor(name, list(shape), dt, kind="Internal").ap()

    x_d = din("x", [NTOK, D])
    flag_d = din("flag", [128, 1])
    ident_d = din("ident", [128, 128])
    cmask_d = din("cmask", [128, 128])
    lvals_d = din("lvals", [64, 17])
    cvals_d = din("cvals", [64, C + 1])
    pre_mix_g = din("pre_mix_g", [D])
    w_in = din("w_in", [D, 3072])
    log_dt = din("ssm_log_dt", [G])
    lam_re = din("ssm_lam_re", [G, PSN])
    lam_im = din("ssm_lam_im", [G, PSN])
    b_re = din("ssm_b_re", [G, PSN, 16])
    b_im = din("ssm_b_im", [G, PSN, 16])
    c_re = din("ssm_c_re", [G, 16, PSN])
    c_im = din("ssm_c_im", [G, 16, PSN])
    ssm_d = din("ssm_d", [DS])
    w_glu = din("ssm_w_glu", [DS, DS])
    conv_w = din("conv_w", [KC, 1024])
    ln_g = din("conv_ln_g", [1024])
    ln_b = din("conv_ln_b", [1024])
    w_out = din("w_out", [D, D])
    post_mix_g = din("post_mix_g", [D])
    pre_ffn_g = din("pre_ffn_g", [D])
    w_up = din("ffn_w_up", [D, 2 * DFF])
    ffn_cw = din("ffn_conv_w", [3, 2 * DFF])
    w_down = din("ffn_w_down", [DFF, D])
    post_ffn_g = din("post_ffn_g", [D])
    out_d = nc.dram_tensor("out", [NMAIN * N, D], F32, kind="ExternalOutput").ap()

    win_b = dscr("win_b", [D, 3072], BF16)
    wglu_b = dscr("wglu_b", [DS, DS], BF16)
    wout_b = dscr("wout_b", [D, D], BF16)
    wup_b = dscr("wup_b", [D, 2 * DFF], BF16)
    wdn_b = dscr("wdn_b", [DFF, D], BF16)
    MI_d = dscr("MI_d", [G, 128, 128], BF16)
    MS_d = dscr("MS_d", [G, 128, 128], BF16)
    MO_d = dscr("MO_d", [G, 2, 64, 128], BF16)
    tab_d = dscr("tab_d", [3, 64, G, C + 1], F32)
    U_scrs = [dscr("U_scr%d" % i_, [8, G, 16, C], BF16) for i_ in range(2)]
    Y_scr = dscr("Y_scr", [8, G, 16, C], F32)
    dbg = {}

    def sb(name, shape, dt=F32):
        return es.enter_context(nc.sbuf_tensor("s_" + name, list(shape), dt))

    pes = contextlib.ExitStack()

    def sbp(name, shape, dt=F32):
        return pes.enter_context(nc.sbuf_tensor("t_" + name, list(shape), dt))

    def ps(name, shape, dt=F32):
        return es.enter_context(nc.psum_tensor("p_" + name, list(shape), dt))

    ident_f = sb("ident_f", [128, 128])
    ident_b = sb("ident_b", [128, 128], BF16)
    ones_f = sb("ones_f", [128, 128])
    ones_b = sb("ones_b", [128, 128], BF16)
    cmask = sb("cmask", [128, 128])
    flag = sb("flagt", [128, 1])
    gpre = sb("gpre", [128, 16])
    gpre2 = sb("gpre2", [128, 16])
    gpost = sb("gpost", [128, D])
    w31 = sb("w31", [128, 8, KC])
    lng = sb("lng", [128, 8])
    lnb = sb("lnb", [128, 8])
    w3 = sb("w3", [128, 88, 3])
    tcos = sb("tcos", [128, 32, C + 1])
    tsin = sb("tsin", [128, 32, C + 1])
    rtab = sb("rtab", [128, 32, 1])
    dmy = sb("dmy", [128, 8])
    dmy_d = dscr("dmy_d", [128, 8], F32)
    carry = sb("carry", [128, 2, 32])
    mmps = [ps("mm%d" % i, [128, 512]) for i in range(3)]
    trps = ps("trps", [128, 4, 256], BF16)
    Lre = ps("Lre", [128, 512])
    Lim = ps("Lim", [128, 512])
    Yps = [ps("Yps%d" % i, [128, 512]) for i in range(2)]
    mmi = [0]
    tri = [0]

    def mm_slot():
        i = mmi[0] % 3
        mmi[0] += 1
        return mmps[i], "mm%d" % i

    def tr_slot():
        i = tri[0] % 4
        tri[0] += 1
        return trps[:, i, :], "tr%d" % i

    def dma(eng, out, in_, r, w, sem, slow=False):
        def fn(e):
            if slow:
                return e.dma_start(out=out, in_=in_, allow_slow_non_contiguous=True)
            return e.dma_start(out=out, in_=in_)
        return P.op(eng, fn, r=r, w=w, dma=sem)

    def gdma(out, in_, key, slow=False):
        P.auto += 1
        o = dma("sp", out, in_, [], ["_g%d" % P.auto], "cg*", slow=slow)
        P.res[key] = [o, {}, []]
        return o

    def act_op(out, in_, func, r, w, scale=None, bias=None, accum=None):
        kw = {}
        if scale is not None:
            kw["scale"] = scale
        if bias is not None:
            kw["bias"] = bias
        if accum is not None:
            kw["accum_out"] = accum
        return P.op("act", lambda e: e.activation(out=out, in_=in_, func=func, **kw), r=r, w=w)

    def tt(eng, out, in0, in1, op, r, w):
        return P.op(eng, lambda e: e.tensor_tensor(out=out, in0=in0, in1=in1, op=op), r=r, w=w)

    def ts(eng, out, in0, s1, s2, op0, op1, r, w):
        if op1 is None:
            return P.op(eng, lambda e: e.tensor_scalar(out=out, in0=in0, scalar1=s1, scalar2=None, op0=op0), r=r, w=w)
        return P.op(eng, lambda e: e.tensor_scalar(out=out, in0=in0, scalar1=s1, scalar2=s2, op0=op0, op1=op1), r=r, w=w)

    def stt(out, in0, scalar, in1, op0, op1, r, w):
        return P.op("dve", lambda e: e.scalar_tensor_tensor(out=out, in0=in0, scalar=scalar, in1=in1, op0=op0, op1=op1), r=r, w=w)

    def mm(out, lhsT, rhs, start, stop, r, w):
        return P.op("pe", lambda e: e.matmul(out, lhsT=lhsT, rhs=rhs, start=start, stop=stop), r=r, w=w)

    def tr(out, in_, idn, r, w):
        return P.op("pe", lambda e: e.transpose(out, in_, idn), r=r, w=w)

    def cp(eng, out, in_, r, w):
        if eng == "act":
            return P.op("act", lambda e: e.copy(out=out, in_=in_), r=r, w=w)
        return P.op(eng, lambda e: e.tensor_copy(out=out, in_=in_), r=r, w=w)

    def split_cast(dst, src, nsplit, sem):
        rows = src.shape[0]
        step = rows // nsplit
        for i in range(nsplit):
            dma("pool", dst[i * step:(i + 1) * step, :], src[i * step:(i + 1) * step, :], [], [sem], sem)

    gdma(ident_f[:], ident_d, "ident_f")
    gdma(cmask[:], cmask_d, "cmask")
    gdma(flag[:], flag_d, "flag")
    gdma(gpre[:], pre_mix_g.rearrange("(k p) -> p k", p=128), "gpre", slow=True)
    gdma(gpre2[:], pre_ffn_g.rearrange("(k p) -> p k", p=128), "gpre2", slow=True)
    gdma(gpost[:], post_mix_g.partition_broadcast(128), "gpost")
    gdma(lng[:], ln_g.rearrange("(k p) -> p k", p=128), "lng", slow=True)
    gdma(lnb[:], ln_b.rearrange("(k p) -> p k", p=128), "lnb", slow=True)
    for k in range(8):
        gdma(w31[:, k, :], conv_w[:, k * 128:(k + 1) * 128].rearrange("t p -> p t"), "w31", slow=True)
    for t in range(3):
        for hh in range(4):
            gdma(w3[:, hh * 22:(hh + 1) * 22, t],
                 ffn_cw[t, hh * 2816:(hh + 1) * 2816].rearrange("(c p) -> p c", p=128), "w3", slow=True)
    cp("dve", ident_b[:], ident_f[:], ["ident_f"], ["ident_b"])
    P.op("pool", lambda e: e.memset(ones_f[:], 1.0), w=["ones_f"])
    P.op("pool", lambda e: e.memset(ones_b[:], 1.0), w=["ones_b"])
    P.op("pool", lambda e: e.memset(carry[:], 0.0), w=["carry"])
    P.op("pool", lambda e: e.memset(dmy[:], 0.0), w=["dmy"])
    split_cast(win_b, w_in, 4, "w_in")
    STOP = os.environ.get("KSTOP", "")
    if STOP == "p0":
        P.emit()
        es.close()
        return nc

    lamr = sbp("lamr", [64, G])
    lami = sbp("lami", [64, G])
    dtt = sbp("dtt", [64, G])
    lv = sbp("lv", [64, 17])
    cv = sbp("cv", [64, C + 1])
    gdma(lamr[:], lam_re.rearrange("g p -> p g"), "lamr", slow=True)
    gdma(lami[:], lam_im.rearrange("g p -> p g"), "lami", slow=True)
    gdma(dtt[:], log_dt.partition_broadcast(64), "dtt")
    gdma(lv[:], lvals_d, "lv")
    gdma(cv[:], cvals_d, "cv")
    bre = sbp("bre", [64, G, 16])
    bim = sbp("bim", [64, G, 16])
    gdma(bre[:], b_re.rearrange("g p h -> p g h"), "bre")
    gdma(bim[:], b_im.rearrange("g p h -> p g h"), "bim")
    cnat = sbp("cnat", [128, 2, 8, 64])
    gdma(cnat[:, 0], c_re.rearrange("(k g) h p -> (g h) k p", g=8), "cnat")
    gdma(cnat[:, 1], c_im.rearrange("(k g) h p -> (g h) k p", g=8), "cnat")
    dcol = sbp("dcol", [128, G])
    for i in range(8):
        gdma(dcol[16 * i:16 * i + 16, :], ssm_d.rearrange("(g h) -> h g", h=16), "dcol", slow=True)
    cT = sbp("cT", [64, 2, G, 16])
    for comp in range(2):
        for k in range(8):
            pt, pk = mm_slot()
            tr(pt[0:64, 0:128], cnat[:, comp, k, :], ident_f[:], ["cnat", "ident_f"], [pk])
            cp("act", cT[:, comp, 8 * k:8 * k + 8, :], pt[0:64, 0:128].rearrange("p (g h) -> p g h", h=16), [pk], ["cT"])
    act_op(dtt[:], dtt[:], AF.Exp, ["dtt"], ["dtt"])
    lrd = sbp("lrd", [64, G])
    lid = sbp("lid", [64, G])
    tt("dve", lrd[:], lamr[:], dtt[:], ALU.mult, ["lamr", "dtt"], ["lrd"])
    tt("dve", lid[:], lami[:], dtt[:], ALU.mult, ["lami", "dtt"], ["lid"])
    TWO_PI = 2.0 * math.pi
    MAGIC = 12582912.0

    def sincos(dst_c, dst_s, mag, ang, shape, key):
        t1 = sbp("sc1_" + key, shape)
        t2 = sbp("sc2_" + key, shape)
        for which, dst in ((0, dst_s), (1, dst_c)):
            src = ang
            if which == 1:
                ts("dve", t2[:], ang, math.pi / 2, None, ALU.add, None, [key + "ang"], [key + "t2"])
                src = t2[:]
            rk = [key + "ang", key + "t2"]
            ts("dve", t1[:], src, 1.0 / TWO_PI, MAGIC, ALU.mult, ALU.add, rk, [key + "t1"])
            ts("dve", t1[:], t1[:], -MAGIC, None, ALU.add, None, [key + "t1"], [key + "t1"])
            stt(t1[:], t1[:], -TWO_PI, src, ALU.mult, ALU.add, rk + [key + "t1"], [key + "t1"])
            ts("dve", t1[:], t1[:], math.pi, -math.pi, ALU.min, ALU.max, [key + "t1"], [key + "t1"])
            act_op(t1[:], t1[:], AF.Sin, [key + "t1"], [key + "t1"])
            tt("dve", dst, t1[:], mag, ALU.mult, [key + "t1", key + "mag"], [key + "dst%d" % which])

    ang = sbp("ang", [64, G, 17])
    mag = sbp("mag", [64, G, 17])
    apr = sbp("apr", [64, G, 17])
    api = sbp("api", [64, G, 17])
    lvb = lv[:, :].unsqueeze(1).broadcast_to((64, G, 17))
    tt("dve", ang[:], lid[:, :].unsqueeze(2).broadcast_to((64, G, 17)), lvb, ALU.mult, ["lid", "lv"], ["Aang"])
    tt("dve", mag[:], lrd[:, :].unsqueeze(2).broadcast_to((64, G, 17)), lvb, ALU.mult, ["lrd", "lv"], ["Amag0"])
    act_op(mag[:], mag[:], AF.Exp, ["Amag0"], ["Amag"])
    sincos(apr[:], api[:], mag[:], ang[:], [64, G, 17], "A")
    tang = sbp("tang", [64, G, C + 1])
    tone = sbp("tone", [64, G, C + 1])
    tco = sbp("tco", [64, G, C + 1])
    tsi = sbp("tsi", [64, G, C + 1])
    cvb = cv[:, :].unsqueeze(1).broadcast_to((64, G, C + 1))
    stt(tang[:], lid[:, :].unsqueeze(2).broadcast_to((64, G, C + 1)), 8.0, cvb, ALU.mult, ALU.mult, ["lid", "cv"], ["Tang"])
    P.op("pool", lambda e: e.memset(tone[:], 1.0), w=["Tmag"])
    sincos(tco[:], tsi[:], tone[:], tang[:], [64, G, C + 1], "T")
    dma("sp", tab_d[0], tco[:], ["Tdst1"], ["tab_d0"], "t0")
    dma("sp", tab_d[1], tsi[:], ["Tdst0"], ["tab_d1"], "t1")
    rt = sbp("rt", [64, G, C + 1])
    ts("dve", rt[:], lrd[:, :].unsqueeze(2).broadcast_to((64, G, C + 1)), 8.0, None, ALU.mult, None, ["lrd"], ["rt"])
    act_op(rt[:], rt[:], AF.Exp, ["rt"], ["rt"])
    dma("sp", tab_d[2], rt[:], ["rt"], ["tab_d2"], "t2")
    for comp, dst in ((0, tcos), (1, tsin), (2, rtab)):
        for par in range(2):
            cc_ = 1 if comp == 2 else C + 1
            dma("sp", dst[64 * par:64 * par + 64, :, :],
                tab_d[comp].rearrange("p (gp par) c -> par p gp c", par=2)[par][:, :, 0:cc_], ["tab_d%d" % comp], ["tabs%d%d" % (comp, par)], "tr%d%d" % (comp, par), slow=(comp == 2))
    TABS = ["tabs%d%d" % (c_, p_) for c_ in range(3) for p_ in range(2)]
    am1 = sbp("am1", [64, G])
    den = sbp("den", [64, G])
    wre = sbp("wre", [64, G])
    wim = sbp("wim", [64, G])
    t64a = sbp("t64a", [64, G])
    t64b = sbp("t64b", [64, G])
    a1r = apr[:, :, 9]
    a1i = api[:, :, 9]
    KA = ["Adst0", "Adst1"]
    ts("dve", am1[:], a1r, -1.0, None, ALU.add, None, KA, ["am1"])
    tt("dve", den[:], lamr[:], lamr[:], ALU.mult, ["lamr"], ["den"])
    tt("dve", t64a[:], lami[:], lami[:], ALU.mult, ["lami"], ["t64a"])
    tt("dve", den[:], den[:], t64a[:], ALU.add, ["den", "t64a"], ["den"])
    P.op("dve", lambda e: e.reciprocal(out=den[:], in_=den[:]), r=["den"], w=["den"])
    tt("dve", wre[:], am1[:], lamr[:], ALU.mult, ["am1", "lamr"], ["wre"])
    tt("dve", t64a[:], a1i, lami[:], ALU.mult, KA + ["lami"], ["t64a"])
    tt("dve", wre[:], wre[:], t64a[:], ALU.add, ["wre", "t64a"], ["wre"])
    tt("dve", wre[:], wre[:], den[:], ALU.mult, ["wre", "den"], ["wre"])
    tt("dve", wim[:], a1i, lamr[:], ALU.mult, KA + ["lamr"], ["wim"])
    tt("dve", t64b[:], am1[:], lami[:], ALU.mult, ["am1", "lami"], ["t64b"])
    tt("dve", wim[:], wim[:], t64b[:], ALU.subtract, ["wim", "t64b"], ["wim"])
    tt("dve", wim[:], wim[:], den[:], ALU.mult, ["wim", "den"], ["wim"])
    bbr = sbp("bbr", [64, G, 16])
    bbi = sbp("bbi", [64, G, 16])
    t16 = sbp("t16", [64, G, 16])
    wreb = wre[:, :].unsqueeze(2).broadcast_to((64, G, 16))
    wimb = wim[:, :].unsqueeze(2).broadcast_to((64, G, 16))
    tt("dve", bbr[:], bre[:], wreb, ALU.mult, ["bre", "wre"], ["bbr"])
    tt("dve", t16[:], bim[:], wimb, ALU.mult, ["bim", "wim"], ["t16"])
    tt("dve", bbr[:], bbr[:], t16[:], ALU.subtract, ["bbr", "t16"], ["bbr"])
    tt("dve", bbi[:], bim[:], wreb, ALU.mult, ["bim", "wre"], ["bbi"])
    tt("dve", t16[:], bre[:], wimb, ALU.mult, ["bre", "wim"], ["t16"])
    tt("dve", bbi[:], bbi[:], t16[:], ALU.add, ["bbi", "t16"], ["bbi"])

    GB = 8
    Hre = sbp("Hre", [64, GB, 8, 16])
    Him = sbp("Him", [64, GB, 8, 16])
    Gre = sbp("Gre", [64, GB, 8, 16])
    Gim = sbp("Gim", [64, GB, 8, 16])
    Ere = sbp("Ere", [64, GB, 8, 16])
    Ein = sbp("Ein", [64, GB, 8, 16])
    tq1 = sbp("tq1", [64, GB, 8, 16])
    MOst = sbp("MOst", [64, GB, 2, 128], BF16)
    MIst = sbp("MIst", [128, GB, 128], BF16)
    MSst = sbp("MSst", [128, GB, 128], BF16)
    tmi = sbp("tmi", [128, 128])
    SH = (64, GB, 8, 16)

    def cmul(dre, dim_, pr, pi, vr, vi, kd, neg_im=False):
        tt("dve", dre, pr, vr, ALU.mult, KA + ["bbr", "bbi", "cT"], [kd + "r"])
        tt("dve", tq1[:], pi, vi, ALU.mult, KA + ["bbr", "bbi", "cT"], ["tq1"])
        tt("dve", dre, dre, tq1[:], ALU.subtract, [kd + "r", "tq1"], [kd + "r"])
        tt("dve", dim_, pr, vi, ALU.mult, KA + ["bbr", "bbi", "cT"], [kd + "i"])
        tt("dve", tq1[:], pi, vr, ALU.mult, KA + ["bbr", "bbi", "cT"], ["tq1"])
        if neg_im:
            stt(dim_, dim_, -1.0, tq1[:], ALU.mult, ALU.subtract, [kd + "i", "tq1"], [kd + "i"])
        else:
            tt("dve", dim_, dim_, tq1[:], ALU.add, [kd + "i", "tq1"], [kd + "i"])

    for bt in range(G // GB):
        g0 = bt * GB
        gs = slice(g0, g0 + GB)
        def pw(t, lo, hi, rev):
            a = t[:, gs, lo:hi]
            if rev:
                a = t[:, gs, hi - 1:lo - 1 if lo > 0 else None:-1] if False else a
            return a
        for i in range(8):
            for (dr, di, idx) in ((Hre, Him, 7 - i), (Gre, Gim, 15 - i)):
                pr = apr[:, gs, idx:idx + 1].broadcast_to((64, GB, 16))
                pi = api[:, gs, idx:idx + 1].broadcast_to((64, GB, 16))
                kd = "H" if dr is Hre else "Gm"
                o_r, o_i = dr[:, :, i, :], di[:, :, i, :]
                tq = tq1[:, :, i, :]
                tt("dve", o_r, pr, bbr[:, gs, :], ALU.mult, KA + ["bbr"], [kd + "r"])
                tt("dve", tq, pi, bbi[:, gs, :], ALU.mult, KA + ["bbi"], ["tq1"])
                tt("dve", o_r, o_r, tq, ALU.subtract, [kd + "r", "tq1"], [kd + "r"])
                tt("dve", o_i, pr, bbi[:, gs, :], ALU.mult, KA + ["bbi"], [kd + "i"])
                tt("dve", tq, pi, bbr[:, gs, :], ALU.mult, KA + ["bbr"], ["tq1"])
                tt("dve", o_i, o_i, tq, ALU.add, [kd + "i", "tq1"], [kd + "i"])
        pr = apr[:, gs, 9:17].unsqueeze(3).broadcast_to(SH)
        pi = api[:, gs, 9:17].unsqueeze(3).broadcast_to(SH)
        cr = cT[:, 0, gs, :].unsqueeze(2).broadcast_to(SH)
        ci = cT[:, 1, gs, :].unsqueeze(2).broadcast_to(SH)
        cmul(Ere[:], Ein[:], pr, pi, cr, ci, "E", neg_im=True)
        cp("act", MOst[:, :, 0, :], Ere[:].rearrange("p g j h -> p g (j h)"), ["Er"], ["MOst"])
        cp("act", MOst[:, :, 1, :], Ein[:].rearrange("p g j h -> p g (j h)"), ["Ei"], ["MOst"])
        for gl in range(GB):
            g = g0 + gl
            pt, pk = mm_slot()
            mm(pt[:, 0:128], Hre[:, gl].rearrange("p i h -> p (i h)"), Ere[:, gl].rearrange("p j h -> p (j h)"),
               True, False, ["Hr", "Er"], [pk])
            mm(pt[:, 0:128], Him[:, gl].rearrange("p i h -> p (i h)"), Ein[:, gl].rearrange("p j h -> p (j h)"),
               False, True, ["Hi", "Ei"], [pk])
            tt("dve", tmi[:], pt[:, 0:128], cmask[:], ALU.mult, [pk, "cmask"], ["tmi"])
            stt(MIst[:, gl, :], ident_f[:], dcol[:, g:g + 1], tmi[:], ALU.mult, ALU.add, ["ident_f", "dcol", "tmi"], ["MIst"])
            pt2, pk2 = mm_slot()
            tr(pt2[:, 0:64], Gre[:, gl].rearrange("p i h -> p (i h)"), ident_f[0:64, 0:64], ["Gmr", "ident_f"], [pk2])
            tr(pt2[:, 64:128], Gim[:, gl].rearrange("p i h -> p (i h)"), ident_f[0:64, 0:64], ["Gmi", "ident_f"], [pk2])
            cp("act", MSst[:, gl, :], pt2[:, 0:128], [pk2], ["MSst"])
        dma("sp", MI_d[gs].rearrange("g k m -> k g m"), MIst[:], ["MIst"], ["MI_d"], "m0")
        dma("sp", MS_d[gs].rearrange("g k m -> k g m"), MSst[:], ["MSst"], ["MS_d"], "m1")
        dma("sp", MO_d[gs].rearrange("g c p m -> p g c m"), MOst[:], ["MOst"], ["MO_d"], "m2")

    if STOP == "prep":
        P.emit()
        pes.close()
        es.close()
        return nc


    prep_dmas = [o for o in P.ops if o.dsem is not None and o.eng == "sp"]
    jk = []
    for en in ("pe", "act", "dve", "pool", "sp"):
        if en == "pe":
            o = P.op("pe", lambda e: e.matmul(mmps[0][0:8, 0:8], lhsT=ident_f[0:8, 0:8], rhs=ident_f[0:8, 0:8], start=True, stop=True), r=["ident_f"], w=["bar_pe", "mm0"])
        elif en == "act":
            o = P.op("act", lambda e: e.copy(out=dmy[:, 0:1], in_=ident_f[:, 0:1]), r=["ident_f", "dmy"], w=["bar_act"])
        elif en == "dve":
            o = P.op("dve", lambda e: e.tensor_copy(out=dmy[:, 1:2], in_=ident_f[:, 0:1]), r=["ident_f", "dmy"], w=["bar_dve"])
        elif en == "pool":
            o = P.op("pool", lambda e: e.tensor_copy(out=dmy[:, 2:3], in_=ident_f[:, 0:1]), r=["ident_f", "dmy"], w=["bar_pool"])
        else:
            o = P.op("sp", lambda e: e.dma_start(out=dmy_d[:, 0:1], in_=dmy[:, 4:5], allow_slow_non_contiguous=True), r=["dmy"], w=["bar_sp"], dma="bar")
            o.deps = list(o.deps) + [d for d in prep_dmas if not d.dsem.endswith("*") or True]
        jk.append(o)
    bars = ["bar_pe", "bar_act", "bar_dve", "bar_pool", "bar_sp"]
    P.op("pe", lambda e: e.matmul(mmps[0][0:8, 8:16], lhsT=ident_f[0:8, 0:8], rhs=ident_f[0:8, 0:8], start=True, stop=True), r=bars + ["ident_f"], w=["bar2_pe", "mm0"])
    P.op("act", lambda e: e.copy(out=dmy[:, 5:6], in_=ident_f[:, 0:1]), r=bars + ["ident_f"], w=["bar2_act"])
    P.op("dve", lambda e: e.tensor_copy(out=dmy[:, 6:7], in_=ident_f[:, 0:1]), r=bars + ["ident_f"], w=["bar2_dve"])
    P.op("pool", lambda e: e.tensor_copy(out=dmy[:, 7:8], in_=ident_f[:, 0:1]), r=bars + ["ident_f"], w=["bar2_pool"])
    P.op("sp", lambda e: e.dma_start(out=dmy_d[:, 1:2], in_=dmy[:, 4:5], allow_slow_non_contiguous=True), r=bars + ["dmy"], w=["bar2_sp"], dma="bar2")
    P.emit(final=False)
    pes.close()

    split_cast(wglu_b, w_glu, 2, "w_glu")
    split_cast(wout_b, w_out, 4, "w_out")
    split_cast(wup_b, w_up, 8, "w_up")
    split_cast(wdn_b, w_down, 8, "w_down")

    x_tm = sb("x_tm", [128, N // 128 if N >= 128 else 1, D])
    f_tm = sb("f_tm", [128, N // 128 if N >= 128 else 1, D])
    xn_tm = sb("xn_tm", [128, 1, D], BF16)
    stat = sb("stat", [128, 16])
    h_fm = sb("h_fm", [128, 16, N], BF16)
    h2_fm = sb("h2_fm", [128, 16, N + 2], BF16)
    act = sb("act", [128, 44, N], BF16)
    z_ext = sb("z_ext", [128, 8, 32 + N], BF16)
    sig = sb("sig", [128, 2, N])
    rr = sb("rr", [128, 4, N])
    sqsb = sb("sqsb", [128, 2, N])
    hl = sb("hl", [128, 4, 2, N], BF16)
    lnst = sb("lnst", [128, 4, N])
    diag = sb("diag", [128, 8, 128], BF16)
    Dre = sb("Dre", [128, 16, C])
    Dim = sb("Dim", [128, 16, C])
    Xre = sb("Xre", [128, 16, C + 1])
    Xim = sb("Xim", [128, 16, C + 1])
    Sf = sb("Sf", [128, 2, 16, C + 1])
    Sb = sb("Sb", [128, 2, 32, C], BF16)
    tmpA = sb("tmpA", [128, 16, C + 1])
    tmpB = sb("tmpB", [128, 16, C + 1])
    Ysb = sb("Ysb", [128, G, C])
    convsb = Ysb[:, :, :].rearrange("p g c -> p (g c)").rearrange("p (k n) -> p k n", n=N)
    MIs = sb("MIs", [128, 2, 8, 128], BF16)
    MSs = sb("MSs", [128, 2, 8, 128], BF16)
    MOs = sb("MOs", [128, 2, 4, 2, 128], BF16)
    WB = 256
    wring = sb("wring", [128, 3, 16, WB], BF16)
    DBW = 128
    dring = sb("dring", [128, 2, 44, DBW], BF16)

    cat = act[:, 0:16, :]
    u_perms = [act[:, 16:24, :], act[:, 8:16, :]]
    U_pks = [act[:, 24:32, :].rearrange("p k (g c) -> p (k g) c", c=C),
             act[:, 32:40, :].rearrange("p k (g c) -> p (k g) c", c=C)]
    gy = act[:, 32:40, :]
    y_fm = f_tm[:, 0, :].rearrange("p (k n) -> p k n", n=N) if N * 8 <= D else None


    P.op("pool", lambda e: e.memset(z_ext[:], 0.0), w=["z_ext"] + ["z%d" % k_ for k_ in range(8)])
    P.op("pool", lambda e: e.memset(h2_fm[:], 0.0), w=["h2_fm", "h2halo"] + ["h2_fm%d" % k_ for k_ in range(16)])
    wslot = [0]
    dslot = [0]
    mslot = [0]

    def wload(src_ap, key, kchunks):
        s = wslot[0] % 3
        wslot[0] += 1
        rk = "wr%d" % s
        dma("sp", wring[:, s, 0:kchunks, :], src_ap.rearrange("(k p) n -> p k n", p=128), [key], [rk], rk)
        return wring[:, s], rk

    def tile(tok0, NT, kind, phase="ALL", ub=0):
        UPK = ["U_pk%d_%d" % (ub, i_) for i_ in range(8)]
        u_perm, U_pk, U_scr = u_perms[ub], U_pks[ub], U_scrs[ub]
        ubase = 16 if ub == 0 else 8
        pbase = 24 if ub == 0 else 32
        CT = NT // 8
        TB = [(o, min(128, NT - o)) for o in range(0, NT, 128)]
        if phase != "SSM":
            for bi, (o, sz) in enumerate(TB):
                dma("sp", x_tm[0:sz, bi, :], x_d[tok0 + o:tok0 + o + sz, :], [], ["x_tm%d" % bi], "xl%d" % bi)

            def norm_to_fm(src_tm, dst_fm, col0, gp, srckeys, kpre):
                for bi, (o, sz) in enumerate(TB):
                    act_op(xn_tm[0:sz, 0, :], src_tm[0:sz, bi, :], AF.Square, [srckeys[bi]], ["xn0", "xnst%d" % bi],
                           accum=stat[0:sz, bi:bi + 1])
                    ts("dve", stat[0:sz, 4 + bi:5 + bi], stat[0:sz, bi:bi + 1], 1.0 / D, EPS, ALU.mult, ALU.add,
                       ["xnst%d" % bi], ["st%d" % bi])
                    act_op(stat[0:sz, 4 + bi:5 + bi], stat[0:sz, 4 + bi:5 + bi], AF.Sqrt, ["st%d" % bi], ["st%d" % bi])
                    P.op("dve", lambda e, bi=bi, sz=sz: e.reciprocal(out=stat[0:sz, 8 + bi:9 + bi], in_=stat[0:sz, 4 + bi:5 + bi]),
                         r=["st%d" % bi], w=["rs%d" % bi])
                    act_op(f_tm[0:sz, bi, :], src_tm[0:sz, bi, :], AF.Identity, [srckeys[bi], "rs%d" % bi],
                           ["f_tm%d" % bi] + (["y_fm%d" % k_ for k_ in range(8)] if bi == 0 else []),
                           scale=stat[0:sz, 8 + bi:9 + bi])
                if STOP == "A0":
                    raise StopBuild()
                for kc in range(16):
                    pt, pk = mm_slot()
                    for bi, (o, sz) in enumerate(TB):
                        tr(pt[:, o:o + sz], f_tm[0:sz, bi, kc * 128:(kc + 1) * 128], ident_f[0:sz, 0:sz],
                           ["f_tm%d" % bi, "ident_f"], [pk])
                    ts("dve", dst_fm[:, kc, col0:col0 + NT], pt[:, 0:NT], gp[:, kc:kc + 1], None, ALU.mult, None,
                       [pk, "gpre", "gpre2"], [kpre + "%d" % kc])

            norm_to_fm(x_tm, h_fm, 0, gpre, ["x_tm%d" % bi for bi in range(len(TB))], "h_fm")
            hk = ["h_fm%d" % kc for kc in range(16)]
            if STOP == "A":
                raise StopBuild()

            def proj_chunk(wt, cofs, rhs_fm, ncol, rkeys, kch=16):
                pt, pk = mm_slot()
                for k in range(kch):
                    mm(pt[:, 0:ncol], wt[:, k, cofs:cofs + 128], rhs_fm(k), k == 0, k == kch - 1, rkeys(k), [pk])
                return pt, pk

            for blk in range(4):
                wt, wk = wload(win_b[:, blk * WB:(blk + 1) * WB], "w_in", 16)
                for c2 in range(2):
                    m = blk * 2 + c2
                    pt, pk = proj_chunk(wt, c2 * 128, lambda k: h_fm[:, k, 0:NT], NT, lambda k: [wk, hk[k]])
                    cp("act", u_perm[:, m, 0:NT].rearrange("p (i c) -> p c i", i=8),
                       pt[:, 0:NT].rearrange("p (c i) -> p c i", i=8), [pk], ["act%d" % (ubase + m)])
                    dma("sp", U_scr[:, 8 * m:8 * m + 8, :, 0:CT].rearrange("i g h c -> (g h) i c"),
                        u_perm[:, m, 0:NT].rearrange("p (i c) -> p i c", i=8), ["act%d" % (ubase + m)], ["U_scr%d_%d" % (ub, m)], "ub%d_%d" % (ub, m))
            for i in range(8):
                dma("sp", U_pk[16 * i:16 * i + 16, :, 0:CT], U_scr[i, :, :, 0:CT].rearrange("g h c -> h g c"),
                    ["U_scr%d_%d" % (ub, m_) for m_ in range(8)], ["U_pk%d_%d" % (ub, i)] + (["act%d" % c_ for c_ in range(pbase, pbase + 8)] if i == 0 else []), "ur%d_%d" % (ub, i))
            if STOP == "B":
                raise StopBuild()
            if kind != "prefix":
                for q in range(4):
                    wv, wvk = wload(win_b[:, 1024 + q * WB:1024 + (q + 1) * WB], "w_in", 16)
                    wg, wgk = wload(win_b[:, 2048 + q * WB:2048 + (q + 1) * WB], "w_in", 16)
                    for c2 in range(2):
                        kk = q * 2 + c2
                        pg, pgk = proj_chunk(wg, c2 * 128, lambda k: h_fm[:, k, 0:NT], NT, lambda k: [wgk, hk[k]])
                        act_op(sig[:, kk % 2, 0:NT], pg[:, 0:NT], AF.Sigmoid, [pgk], ["sig%d" % (kk % 2)])
                        pv, pvk = proj_chunk(wv, c2 * 128, lambda k: h_fm[:, k, 0:NT], NT, lambda k: [wvk, hk[k]])
                        tt("dve", z_ext[:, kk, 32:32 + NT], pv[:, 0:NT], sig[:, kk % 2, 0:NT], ALU.mult,
                           [pvk, "sig%d" % (kk % 2)], ["z%d" % kk])

        if phase == "AB":
            return
        for half in range(2):
            gb0 = half * 32
            for sub in range(4):
                s = mslot[0] % 2
                mslot[0] += 1
                gsub = slice(gb0 + sub * 8, gb0 + sub * 8 + 8)
                dma("sp", MSs[:, s], MS_d[gsub].rearrange("g k m -> k g m"), ["MS_d"], ["MSs%d" % s], "ms%d" % s)
                for gl in range(8):
                    g = gb0 + sub * 8 + gl
                    par, gpl = g % 2, (g - gb0) // 2
                    mm(Lre[64 * par:64 * par + 64, gpl * CT:(gpl + 1) * CT], MSs[:, s, gl, 0:64], U_pk[:, g, 0:CT],
                       True, True, ["MSs%d" % s, ] + UPK, ["Lre"])
                    mm(Lim[64 * par:64 * par + 64, gpl * CT:(gpl + 1) * CT], MSs[:, s, gl, 64:128], U_pk[:, g, 0:CT],
                       True, True, ["MSs%d" % s, ] + UPK, ["Lim"])
                if sub == 3 and STOP == "L":
                    raise StopBuild()
                if sub == 3:
                    gps = slice(half * 16, half * 16 + 16)
                    lre = Lre[:, 0:16 * CT].rearrange("p (g c) -> p g c", c=CT)
                    lim = Lim[:, 0:16 * CT].rearrange("p (g c) -> p g c", c=CT)
                    co = tcos[:, gps, 1:CT + 1]
                    si = tsin[:, gps, 1:CT + 1]
                    tA = tmpA[:, :, 0:CT]
                    tB = tmpB[:, :, 0:CT]
                    tt("dve", Dre[:, :, 0:CT], lre, co, ALU.mult, ["Lre", ] + TABS, ["Dre"])
                    tt("dve", tA, lim, si, ALU.mult, ["Lim", ] + TABS, ["tmpA"])
                    tt("dve", Dre[:, :, 0:CT], Dre[:, :, 0:CT], tA, ALU.add, ["Dre", "tmpA"], ["Dre"])
                    tt("dve", Dim[:, :, 0:CT], lim, co, ALU.mult, ["Lim", ] + TABS, ["Dim"])
                    tt("dve", tB, lre, si, ALU.mult, ["Lre", ] + TABS, ["tmpB"])
                    tt("dve", Dim[:, :, 0:CT], Dim[:, :, 0:CT], tB, ALU.subtract, ["Dim", "tmpB"], ["Dim"])
                    cp("dve", Xre[:, :, 0:1], carry[:, 0, gps].unsqueeze(2), ["carry"], ["Xre"])
                    cp("dve", Xim[:, :, 0:1], carry[:, 1, gps].unsqueeze(2), ["carry"], ["Xim"])
                    for gpl in range(16):
                        gp = half * 16 + gpl
                        for (X, Dd, kx, kd_) in ((Xre, Dre, "Xre", "Dre"), (Xim, Dim, "Xim", "Dim")):
                            P.op("dve", lambda e, X=X, Dd=Dd, gpl=gpl, gp=gp: e.tensor_tensor_scan(
                                out=X[:, gpl, 1:CT + 1], data0=rtab[:, gp, 0:1].broadcast_to((128, CT)), data1=Dd[:, gpl, 0:CT],
                                initial=X[:, gpl, 0:1], op0=ALU.mult, op1=ALU.add),
                                r=[kd_, *TABS, kx], w=[kx])
                    if STOP == "scan":
                        raise StopBuild()
                    co = tcos[:, gps, 0:CT + 1]
                    si = tsin[:, gps, 0:CT + 1]
                    tA = tmpA[:, :, 0:CT + 1]
                    tB = tmpB[:, :, 0:CT + 1]
                    sre = Sf[:, 0, :, 0:CT + 1]
                    sim = Sf[:, 1, :, 0:CT + 1]
                    xr = Xre[:, :, 0:CT + 1]
                    xi = Xim[:, :, 0:CT + 1]
                    seng = "dve" if kind == "prefix" else "pool"
                    tt(seng, sre, xr, co, ALU.mult, ["Xre", ] + TABS, ["Sre"])
                    tt(seng, tA, xi, si, ALU.mult, ["Xim", ] + TABS, ["tmpA"])
                    tt(seng, sre, sre, tA, ALU.subtract, ["Sre", "tmpA"], ["Sre"])
                    tt(seng, sim, xi, co, ALU.mult, ["Xim", ] + TABS, ["Sim"])
                    tt(seng, tB, xr, si, ALU.mult, ["Xre", ] + TABS, ["tmpB"])
                    tt(seng, sim, sim, tB, ALU.add, ["Sim", "tmpB"], ["Sim"])
                    cp("act", carry[:, 0, gps], Sf[:, 0, :, CT], ["Sre"], ["carry"])
                    cp("act", carry[:, 1, gps], Sf[:, 1, :, CT], ["Sim"], ["carry"])
                    if kind != "prefix":
                        cp("act", Sb[:, 0, gps, 0:CT], Sf[:, 0, :, 0:CT], ["Sre"], ["Sb"])
                        cp("act", Sb[:, 1, gps, 0:CT], Sf[:, 1, :, 0:CT], ["Sim"], ["Sb"])
            if kind == "prefix":
                continue
            for sub in range(4):
                s = mslot[0] % 2
                mslot[0] += 1
                gsub = slice(gb0 + sub * 8, gb0 + sub * 8 + 8)
                dma("sp", MIs[:, s], MI_d[gsub].rearrange("g k m -> k g m"), ["MI_d"], ["MIs%d" % s], "mi%d" % s)
                for par in range(2):
                    for comp in range(2):
                        dma("sp", MOs[64 * par:64 * par + 64, s, :, comp, :],
                            MO_d[gsub].rearrange("(gp par) c p m -> par c p gp m", par=2)[par, comp],
                            ["MO_d"], ["MOs%d_%d%d" % (s, par, comp)], "mo%d_%d%d" % (s, par, comp))
                for gl in range(8):
                    g = gb0 + sub * 8 + gl
                    par, gp = g % 2, g // 2
                    gi = g - gb0
                    yp = Yps[gi // 16]
                    yk = "Yps%d" % (gi // 16)
                    cs = slice((gi % 16) * CT, (gi % 16 + 1) * CT)
                    mm(yp[:, cs], MIs[:, s, gl, :], U_pk[:, g, 0:CT], True, False, ["MIs%d" % s, ] + UPK, [yk])
                    mm(yp[:, cs], MOs[64 * par:64 * par + 64, s, gl // 2, 0, :], Sb[64 * par:64 * par + 64, 0, gp, 0:CT],
                       False, False, ["MOs%d_%d%d" % (s, a_, b_) for a_ in range(2) for b_ in range(2)] + ["Sb"], [yk])
                    mm(yp[:, cs], MOs[64 * par:64 * par + 64, s, gl // 2, 1, :], Sb[64 * par:64 * par + 64, 1, gp, 0:CT],
                       False, True, ["MOs%d_%d%d" % (s, a_, b_) for a_ in range(2) for b_ in range(2)] + ["Sb"], [yk])
            for hh in range(2):
                cp("act", Ysb[:, gb0 + 16 * hh:gb0 + 16 * hh + 16, 0:CT],
                   Yps[hh][:, 0:16 * CT].rearrange("p (g c) -> p g c", c=CT), ["Yps%d" % hh], ["Ysb"] + ["cv%d" % k_ for k_ in range(8)])
        if kind == "prefix":
            return
        for j in range(8):
            dma("sp", Y_scr[j, :, :, 0:CT].rearrange("g h c -> h g c"), Ysb[16 * j:16 * j + 16, :, 0:CT],
                ["Ysb"], ["Y_scr%d" % j], "yb%d" % j)
        for k in range(8):
            dma("sp", y_fm[:, k, 0:NT].rearrange("p (j c) -> p j c", j=8),
                Y_scr[:, 8 * k:8 * k + 8, :, 0:CT].rearrange("j g h c -> (g h) j c"),
                ["Y_scr%d" % j_ for j_ in range(8)] + ([] if k == 0 else ["f_tm0"]),
                ["y_fm%d" % k] + (["f_tm0"] if k == 0 else []), "yr%d" % k)

        if kind == "main" and STOP == "m_ssm":
            raise StopBuild()
        dgi = [0]
        sp_, spk = Lre, "Lre"
        sq_, sqk = Lim, "Lim"
        for kk in range(8):
            if kind == "main" and STOP == "c1b" and kk == 1:
                raise StopBuild()
            if kind == "main" and STOP == "c1c" and kk == 2:
                raise StopBuild()
            pt, pk = mm_slot()
            for t in range(KC):
                ds_ = dgi[0] % 8
                dgi[0] += 1
                if t % 2 == 0:
                    act_op(diag[:, ds_, :], ident_b[:], AF.Identity, ["ident_b", "w31"], ["dg%d" % ds_],
                           scale=w31[:, kk, t:t + 1])
                else:
                    ts("dve", diag[:, ds_, :], ident_b[:], w31[:, kk, t:t + 1], None, ALU.mult, None,
                       ["ident_b", "w31"], ["dg%d" % ds_])
                tofs = 2 + t - ((t % 2) if os.environ.get("KEVEN") else 0)
                mm(pt[:, 0:NT], diag[:, ds_, :], z_ext[:, kk, tofs:tofs + NT], t == 0, t == KC - 1,
                   ["dg%d" % ds_, "z%d" % kk, "z_ext"], [pk])
            if kind == "main" and STOP == "c1":
                raise StopBuild()
            cp("act", convsb[:, kk, 0:NT], pt[:, 0:NT], [pk], ["cv%d" % kk, "Ysb"])
            if kind == "main" and STOP == "c1d":
                raise StopBuild()
            b2 = kk % 2
            act_op(sqsb[:, b2, 0:NT], pt[:, 0:NT], AF.Square, [pk], ["sq%d" % b2])
            cp("act", hl[:, 0, b2, 0:NT], pt[:, 0:NT], [pk], ["chi%d" % b2])
            tt("dve", hl[:, 1, b2, 0:NT], convsb[:, kk, 0:NT], hl[:, 0, b2, 0:NT], ALU.subtract,
               ["cv%d" % kk, "chi%d" % b2], ["clo%d" % b2])
            cp("act", hl[:, 2, b2, 0:NT], sqsb[:, b2, 0:NT], ["sq%d" % b2], ["shi%d" % b2])
            tt("dve", hl[:, 3, b2, 0:NT], sqsb[:, b2, 0:NT], hl[:, 2, b2, 0:NT], ALU.subtract,
               ["sq%d" % b2, "shi%d" % b2], ["slo%d" % b2])
            if kind == "main" and STOP == "c1f":
                raise StopBuild()
            mm(sp_[:, 0:NT], ones_b[:], hl[:, 0, b2, 0:NT], kk == 0, False, ["ones_b", "chi%d" % b2], [spk])
            mm(sp_[:, 0:NT], ones_b[:], hl[:, 1, b2, 0:NT], False, kk == 7, ["ones_b", "clo%d" % b2], [spk])
            mm(sq_[:, 0:NT], ones_b[:], hl[:, 2, b2, 0:NT], kk == 0, False, ["ones_b", "shi%d" % b2], [sqk])
            mm(sq_[:, 0:NT], ones_b[:], hl[:, 3, b2, 0:NT], False, kk == 7, ["ones_b", "slo%d" % b2], [sqk])
        if kind == "main" and STOP == "c2":
            raise StopBuild()
        mean = lnst[:, 0, 0:NT]
        var = lnst[:, 1, 0:NT]
        rstd = lnst[:, 2, 0:NT]
        act_op(mean, sp_[:, 0:NT], AF.Copy, [spk], ["ln_m"], scale=1.0 / 1024)
        tt("dve", var, mean, mean, ALU.mult, ["ln_m"], ["ln_v"])
        stt(var, sq_[:, 0:NT], 1.0 / 1024, var, ALU.mult, ALU.subtract, [sqk, "ln_v"], ["ln_v"])
        ts("dve", var, var, EPS, None, ALU.add, None, ["ln_v"], ["ln_v"])
        act_op(var, var, AF.Sqrt, ["ln_v"], ["ln_v"])
        P.op("dve", lambda e: e.reciprocal(out=rstd, in_=var), r=["ln_v"], w=["ln_r"])
        for kk in range(8):
            tt("dve", convsb[:, kk, 0:NT], convsb[:, kk, 0:NT], mean, ALU.subtract, ["cv%d" % kk, "ln_m"], ["cv%d" % kk])
            tt("dve", convsb[:, kk, 0:NT], convsb[:, kk, 0:NT], rstd, ALU.mult, ["cv%d" % kk, "ln_r"], ["cv%d" % kk])
            act_op(cat[:, 8 + kk, 0:NT], convsb[:, kk, 0:NT], AF.Silu, ["cv%d" % kk, "lng", "lnb"], ["act%d" % (8 + kk)],
                   scale=lng[:, kk:kk + 1], bias=lnb[:, kk:kk + 1])
        cp("pool", z_ext[:, :, 0:32], z_ext[:, :, NT:NT + 32], ["z%d" % kk for kk in range(8)] + ["z_ext"],
           ["z_ext"] + ["z%d" % kk for kk in range(8)])

        if kind == "main" and STOP == "m_conv":
            raise StopBuild()
        for k in range(8):
            act_op(gy[:, k, 0:NT], y_fm[:, k, 0:NT], AF.Gelu_apprx_tanh, ["y_fm%d" % k], ["act%d" % (32 + k)])
        for blk in range(4):
            wt, wk = wload(wglu_b[:, blk * WB:(blk + 1) * WB], "w_glu", 8)
            for c2 in range(2):
                m = blk * 2 + c2
                pt, pk = proj_chunk(wt, c2 * 128, lambda k: gy[:, k, 0:NT], NT, lambda k: [wk, "act%d" % (32 + k)], kch=8)
                act_op(sig[:, m % 2, 0:NT], pt[:, 0:NT], AF.Sigmoid, [pk], ["sig%d" % (m % 2)])
                act_op(rr[:, m % 2, 0:NT], y_fm[:, m, 0:NT], AF.Gelu_apprx_tanh, ["y_fm%d" % m], ["rr%d" % (m % 2)])
                tt("dve", cat[:, m, 0:NT].rearrange("p (c j) -> p j c", j=8),
                   rr[:, m % 2, 0:NT].rearrange("p (j c) -> p j c", j=8),
                   sig[:, m % 2, 0:NT].rearrange("p (j c) -> p j c", j=8), ALU.mult,
                   ["rr%d" % (m % 2), "sig%d" % (m % 2)], ["act%d" % m])

        if kind == "main" and STOP == "m_glu":
            raise StopBuild()
        for nb in range(D // WB):
            wt, wk = wload(wout_b[:, nb * WB:(nb + 1) * WB], "w_out", 16)
            for bi, (o, sz) in enumerate(TB):
                pt, pk = mm_slot()
                for k in range(16):
                    mm(pt[0:sz, 0:WB], cat[:, k, o:o + sz], wt[:, k, :], k == 0, k == 15, ["act%d" % k, wk], [pk])
                cp("act", f_tm[0:sz, bi, nb * WB:(nb + 1) * WB], pt[0:sz, 0:WB], [pk], ["f_tm%d" % bi] + (["y_fm%d" % k_ for k_ in range(8)] if bi == 0 else []))

        def post_norm_add(gtile, gkey):
            for bi, (o, sz) in enumerate(TB):
                act_op(xn_tm[0:sz, 0, :], f_tm[0:sz, bi, :], AF.Square, ["f_tm%d" % bi], ["xn0", "xnst%d" % bi],
                       accum=stat[0:sz, bi:bi + 1])
                ts("dve", stat[0:sz, 4 + bi:5 + bi], stat[0:sz, bi:bi + 1], 1.0 / D, EPS, ALU.mult, ALU.add,
                   ["xnst%d" % bi], ["st%d" % bi])
                act_op(stat[0:sz, 4 + bi:5 + bi], stat[0:sz, 4 + bi:5 + bi], AF.Sqrt, ["st%d" % bi], ["st%d" % bi])
                P.op("dve", lambda e, bi=bi, sz=sz: e.reciprocal(out=stat[0:sz, 8 + bi:9 + bi], in_=stat[0:sz, 4 + bi:5 + bi]),
                     r=["st%d" % bi], w=["rs%d" % bi])
                stt(f_tm[0:sz, bi, :], f_tm[0:sz, bi, :], stat[0:sz, 8 + bi:9 + bi], gtile[0:sz, :], ALU.mult, ALU.mult,
                    ["f_tm%d" % bi, "rs%d" % bi, gkey], ["f_tm%d" % bi])
                tt("dve", x_tm[0:sz, bi, :], x_tm[0:sz, bi, :], f_tm[0:sz, bi, :], ALU.add,
                   ["x_tm%d" % bi, "f_tm%d" % bi], ["x_tm%d" % bi])

        post_norm_add(gpost, "gpost")
        if kind == "main":
            dma("sp", gpost[:], post_ffn_g.partition_broadcast(128), [], ["gpost"], "gp")
        if kind == "main":
            pass
        norm_to_fm(x_tm, h2_fm, 2, gpre2, ["x_tm%d" % bi for bi in range(len(TB))], "h2_fm")
        h2k = ["h2_fm%d" % kc for kc in range(16)]
        if kind == "mini":
            ts("dve", h2_fm[:, :, 0:2], h2_fm[:, :, NT:NT + 2], flag[:, 0:1], None, ALU.mult, None,
               h2k + ["flag"], ["h2_fm", "h2halo"])
            return
        if kind == "main" and STOP == "m_wout":
            raise StopBuild()
        for q in range(22):
            wg, wgk = wload(wup_b[:, q * WB:(q + 1) * WB], "w_up", 16)
            wv, wvk = wload(wup_b[:, DFF + q * WB:DFF + (q + 1) * WB], "w_up", 16)
            for c2 in range(2):
                ch = q * 2 + c2
                res = []
                for (wt, wk, cidx, slot) in ((wg, wgk, ch, 0), (wv, wvk, 44 + ch, 1)):
                    pt, pk = proj_chunk(wt, c2 * 128, lambda k: h2_fm[:, k, 0:NT + 2], NT + 2,
                                        lambda k: [wk, h2k[k], "h2halo", "h2_fm"])
                    r_ = rr[:, 2 * (ch % 2) + slot, 0:NT]
                    rk_ = "rr%d" % (2 * (ch % 2) + slot)
                    act_op(r_, pt[:, 0:NT], AF.Identity, [pk, "w3"], [rk_], scale=w3[:, cidx, 0:1])
                    stt(r_, pt[:, 1:NT + 1], w3[:, cidx, 1:2], r_, ALU.mult, ALU.add, [pk, "w3", rk_], [rk_])
                    stt(r_, pt[:, 2:NT + 2], w3[:, cidx, 2:3], r_, ALU.mult, ALU.add, [pk, "w3", rk_], [rk_])
                    res.append((r_, rk_))
                (rg, rgk), (rv, rvk) = res
                act_op(rg, rg, AF.Gelu_apprx_tanh, [rgk], [rgk])
                tt("dve", act[:, ch, 0:NT], rg, rv, ALU.mult, [rgk, rvk], ["act%d" % ch])
        cp("pool", h2_fm[:, :, 0:2], h2_fm[:, :, NT:NT + 2], h2k + ["h2_fm"], ["h2_fm", "h2halo"] + h2k)
        if kind == "main" and STOP == "m_up":
            raise StopBuild()
        for nb in range(D // DBW):
            s = dslot[0] % 2
            dslot[0] += 1
            dk = "dr%d" % s
            dma("sp", dring[:, s], wdn_b[:, nb * DBW:(nb + 1) * DBW].rearrange("(k p) n -> p k n", p=128),
                ["w_down"], [dk], dk)
            for bi, (o, sz) in enumerate(TB):
                pt, pk = mm_slot()
                for k in range(44):
                    mm(pt[0:sz, 0:DBW], act[:, k, o:o + sz], dring[:, s, k, :], k == 0, k == 43,
                       ["act%d" % k, dk], [pk])
                cp("act", f_tm[0:sz, bi, nb * DBW:(nb + 1) * DBW], pt[0:sz, 0:DBW], [pk], ["f_tm%d" % bi])
        post_norm_add(gpost, "gpost")
        dma("sp", gpost[:], post_mix_g.partition_broadcast(128), [], ["gpost"], "gp")
        orow = tok0 - (NPRE * N + NMINI)
        for bi, (o, sz) in enumerate(TB):
            dma("sp", out_d[orow + o:orow + o + sz, :], x_tm[0:sz, bi, :], ["x_tm%d" % bi], ["out"], "os")

    t0 = 0
    try:
        if STOP != "bar" and NPRE > 0:
            tile(0, N, "prefix", "AB", 0)
        for i in range(NPRE):
            if STOP == "bar":
                break
            if i + 1 < NPRE:
                tile((i + 1) * N, N, "prefix", "AB", (i + 1) % 2)
            tile(i * N, N, "prefix", "SSM", i % 2)
            t0 += N
    except StopBuild:
        P.emit()
        es.close()
        return nc
    if STOP in ("bar", "prefix"):
        P.emit()
        es.close()
        return nc
    tile(t0, NMINI, "mini")
    if STOP == "mini":
        P.emit()
        es.close()
        return nc
    t0 += NMINI
    try:
        for i in range(NMAIN):
            tile(t0, N, "main")
            t0 += N
    except StopBuild:
        P.emit()
        es.close()
        return nc
    P.emit()
    es.close()
    return nc


def host_consts(C):
    ident = np.eye(128, dtype=np.float32)
    idx = np.arange(128) // 16
    cmask = (idx[:, None] <= idx[None, :]).astype(np.float32)
    lvals = np.tile(np.arange(-8, 9, dtype=np.float32)[None, :], (64, 1))
    cvals = np.tile(np.arange(0, C + 1, dtype=np.float32)[None, :], (64, 1))
    return ident, cmask, lvals, cvals


def run(inputs, NPRE, NMAIN, N=256, NMINI=32, trace=False):
    x = np.asarray(inputs["x"], dtype=np.float32)
    B, S, _ = x.shape
    seg = NMAIN * N
    nseg = S // seg
    assert B * nseg == NCORE
    nc = build(NPRE, NMAIN, N, NMINI)
    ident, cmask, lvals, cvals = host_consts(N // 8)
    shared = {"ident": ident, "cmask": cmask, "lvals": lvals, "cvals": cvals}
    for k, v in inputs.items():
        if k == "x":
            continue
        a = np.asarray(v, dtype=np.float32)
        shared[k] = np.ascontiguousarray(a.reshape(a.shape[1:]))
    pre = NPRE * N + NMINI
    in_maps = []
    for c in range(NCORE):
        b, sg = divmod(c, nseg)
        s0 = sg * seg
        xs = np.zeros((pre + seg, D), np.float32)
        lo = s0 - pre
        src_lo = max(lo, 0)
        xs[src_lo - lo:, :] = x[b, src_lo:s0 + seg, :]
        m = dict(shared)
        m["x"] = xs
        m["flag"] = np.full((128, 1), 1.0 if s0 > 0 else 0.0, np.float32)
        in_maps.append(m)
    res = run_bass_kernel_spmd(nc, in_maps, core_ids=list(range(NCORE)), trace=trace)
    out = np.zeros((B, S, D), np.float32)
    for c in range(NCORE):
        b, sg = divmod(c, nseg)
        out[b, sg * seg:(sg + 1) * seg, :] = res.results[c]["out"]
    return out, res


def kernel(**inputs):
    out, _ = run(inputs, NPRE=24, NMAIN=8)
    return out
```

```python
import contextlib
import os
import math
import numpy as np
import concourse.bass as bass
import concourse.mybir as mybir
from concourse.bass_utils import run_bass_kernel_spmd
from concourse.alu_op_type import AluOpType as ALU

F32 = mybir.dt.float32
BF16 = mybir.dt.bfloat16
AF = mybir.ActivationFunctionType

D = 2048
DS = 1024
G = 64
PSN = 64
DFF = 5632
KC = 31
EPS = 1e-6
NCORE = 8
SAME_ENGINE_SYNC = os.environ.get("KSES", "1") == "1"


class StopBuild(Exception):
    pass


class Op:
    __slots__ = ("eng", "fn", "deps", "sig", "cnt", "dsem", "dval", "idx")


class Prog:
    def __init__(self, nc, es):
        self.nc = nc
        self.es = es
        self.ops = []
        self.res = {}
        self.dcount = {}
        self.dsems = {}
        self.esem = None
        self.cnt = None
        self.waited = None
        self.emitted = 0
        self.auto = 0

    def op(self, eng, fn, r=(), w=(), dma=None):
        o = Op()
        o.eng, o.fn, o.sig, o.cnt, o.dsem, o.dval = eng, fn, False, 0, dma, 0
        o.idx = len(self.ops)
        deps = {}
        for k in r:
            e = self.res.setdefault(k, [None, {}, []])
            if e[0] is not None:
                deps[id(e[0])] = e[0]
        for k in w:
            e = self.res.setdefault(k, [None, {}, []])
            if e[0] is not None:
                deps[id(e[0])] = e[0]
            for x in e[1].values():
                deps[id(x)] = x
            for x in e[2]:
                deps[id(x)] = x
        for k in r:
            e = self.res[k]
            if dma is None:
                e[1][eng] = o
            else:
                e[2].append(o)
        for k in w:
            self.res[k] = [o, {}, []]
        dl = []
        for d in deps.values():
            if d.dsem is None and d.eng == eng:
                if eng in ("pe", "sp") or not SAME_ENGINE_SYNC:
                    continue
            dl.append(d)
        o.deps = dl
        if dma is not None:
            self.dcount[dma] = self.dcount.get(dma, 0) + 16
            o.dval = self.dcount[dma]
        self.ops.append(o)
        return o

    def emit(self, final=True):
        nc = self.nc
        engs = {"pe": nc.tensor, "act": nc.scalar, "dve": nc.vector, "pool": nc.gpsimd, "sp": nc.sync}
        if self.esem is None:
            self.esem = {k: self.es.enter_context(nc.semaphore("es_" + k)) for k in engs}
            self.cnt = {k: 0 for k in engs}
            self.waited = {k: {} for k in engs}
        esem, cnt, waited = self.esem, self.cnt, self.waited
        for k in self.dcount:
            if k not in self.dsems:
                self.dsems[k] = self.es.enter_context(nc.semaphore("ds_" + k.replace("*", "g")))
        todo = self.ops[self.emitted:]
        self.emitted = len(self.ops)
        for o in todo:
            for d in o.deps:
                if d.dsem is None:
                    d.sig = True
        for o in todo:
            if o.dsem is None and o.sig:
                cnt[o.eng] += 1
                o.cnt = cnt[o.eng]
        for o in todo:
            e = engs[o.eng]
            need = {}
            for d in o.deps:
                if d.dsem is None:
                    key, val, sem = "e_" + d.eng, d.cnt, esem[d.eng]
                else:
                    dv = self.dcount[d.dsem] if d.dsem.endswith("*") else d.dval
                    key, val, sem = "d_" + d.dsem, dv, self.dsems[d.dsem]
                if need.get(key, (0, None))[0] < val:
                    need[key] = (val, sem)
            for key, (val, sem) in need.items():
                if waited[o.eng].get(key, 0) < val:
                    e.wait_ge(sem, val)
                    waited[o.eng][key] = val
            ins = o.fn(e)
            if o.dsem is not None:
                ins.then_inc(self.dsems[o.dsem], 16)
            elif o.sig:
                ins.then_inc(esem[o.eng], 1)
        if not final:
            return
        for k, v in self.dcount.items():
            if waited["sp"].get("d_" + k, 0) < v:
                nc.sync.wait_ge(self.dsems[k], v)


def build(NPRE, NMAIN, N=256, NMINI=32, debug=False):
    C = N // 8
    nc = bass.Bass("TRN2", target_bir_lowering=False)
    es = contextlib.ExitStack()
    P = Prog(nc, es)
    NTOK = NPRE * N + NMINI + NMAIN * N

    def din(name, shape, dt=F32):
        return nc.dram_tensor(name, list(shape), dt, kind="ExternalInput").ap()

    def dscr(name, shape, dt):
        return nc.dram_tensor(name, list(shape), dt, kind="Internal").ap()

    x_d = din("x", [NTOK, D])
    flag_d = din("flag", [128, 1])
    ident_d = din("ident", [128, 128])
    cmask_d = din("cmask", [128, 128])
    lvals_d = din("lvals", [64, 17])
    cvals_d = din("cvals", [64, C + 1])
    pre_mix_g = din("pre_mix_g", [D])
    w_in = din("w_in", [D, 3072])
    log_dt = din("ssm_log_dt", [G])
    lam_re = din("ssm_lam_re", [G, PSN])
    lam_im = din("ssm_lam_im", [G, PSN])
    b_re = din("ssm_b_re", [G, PSN, 16])
    b_im = din("ssm_b_im", [G, PSN, 16])
    c_re = din("ssm_c_re", [G, 16, PSN])
    c_im = din("ssm_c_im", [G, 16, PSN])
    ssm_d = din("ssm_d", [DS])
    w_glu = din("ssm_w_glu", [DS, DS])
    conv_w = din("conv_w", [KC, 1024])
    ln_g = din("conv_ln_g", [1024])
    ln_b = din("conv_ln_b", [1024])
    w_out = din("w_out", [D, D])
    post_mix_g = din("post_mix_g", [D])
    pre_ffn_g = din("pre_ffn_g", [D])
    w_up = din("ffn_w_up", [D, 2 * DFF])
    ffn_cw = din("ffn_conv_w", [3, 2 * DFF])
    w_down = din("ffn_w_down", [DFF, D])
    post_ffn_g = din("post_ffn_g", [D])
    out_d = nc.dram_tensor("out", [NMAIN * N, D], F32, kind="ExternalOutput").ap()

    win_b = dscr("win_b", [D, 3072], BF16)
    wglu_b = dscr("wglu_b", [DS, DS], BF16)
    wout_b = dscr("wout_b", [D, D], BF16)
    wup_b = dscr("wup_b", [D, 2 * DFF], BF16)
    wdn_b = dscr("wdn_b", [DFF, D], BF16)
    MI_d = dscr("MI_d", [G, 128, 128], BF16)
    MS_d = dscr("MS_d", [G, 128, 128], BF16)
    MO_d = dscr("MO_d", [G, 2, 64, 128], BF16)
    tab_d = dscr("tab_d", [3, 64, G, C + 1], F32)
    U_scrs = [dscr("U_scr%d" % i_, [8, G, 16, C], BF16) for i_ in range(2)]
    Y_scr = dscr("Y_scr", [8, G, 16, C], F32)
    dbg = {}

    def sb(name, shape, dt=F32):
        return es.enter_context(nc.sbuf_tensor("s_" + name, list(shape), dt))

    pes = contextlib.ExitStack()

    def sbp(name, shape, dt=F32):
        return pes.enter_context(nc.sbuf_tensor("t_" + name, list(shape), dt))

    def ps(name, shape, dt=F32):
        return es.enter_context(nc.psum_tensor("p_" + name, list(shape), dt))

    ident_f = sb("ident_f", [128, 128])
    ident_b = sb("ident_b", [128, 128], BF16)
    ones_f = sb("ones_f", [128, 128])
    ones_b = sb("ones_b", [128, 128], BF16)
    cmask = sb("cmask", [128, 128])
    flag = sb("flagt", [128, 1])
    gpre = sb("gpre", [128, 16])
    gpre2 = sb("gpre2", [128, 16])
    gpost = sb("gpost", [128, D])
    w31 = sb("w31", [128, 8, KC])
    lng = sb("lng", [128, 8])
    lnb = sb("lnb", [128, 8])
    w3 = sb("w3", [128, 88, 3])
    tcos = sb("tcos", [128, 32, C + 1])
    tsin = sb("tsin", [128, 32, C + 1])
    rtab = sb("rtab", [128, 32, 1])
    dmy = sb("dmy", [128, 8])
    dmy_d = dscr("dmy_d", [128, 8], F32)
    carry = sb("carry", [128, 2, 32])
    mmps = [ps("mm%d" % i, [128, 512]) for i in range(3)]
    trps = ps("trps", [128, 4, 256], BF16)
    Lre = ps("Lre", [128, 512])
    Lim = ps("Lim", [128, 512])
    Yps = [ps("Yps%d" % i, [128, 512]) for i in range(2)]
    mmi = [0]
    tri = [0]

    def mm_slot():
        i = mmi[0] % 3
        mmi[0] += 1
        return mmps[i], "mm%d" % i

    def tr_slot():
        i = tri[0] % 4
        tri[0] += 1
        return trps[:, i, :], "tr%d" % i

    def dma(eng, out, in_, r, w, sem, slow=False):
        def fn(e):
            if slow:
                return e.dma_start(out=out, in_=in_, allow_slow_non_contiguous=True)
            return e.dma_start(out=out, in_=in_)
        return P.op(eng, fn, r=r, w=w, dma=sem)

    def gdma(out, in_, key, slow=False):
        P.auto += 1
        o = dma("sp", out, in_, [], ["_g%d" % P.auto], "cg*", slow=slow)
        P.res[key] = [o, {}, []]
        return o

    def act_op(out, in_, func, r, w, scale=None, bias=None, accum=None):
        kw = {}
        if scale is not None:
            kw["scale"] = scale
        if bias is not None:
            kw["bias"] = bias
        if accum is not None:
            kw["accum_out"] = accum
        return P.op("act", lambda e: e.activation(out=out, in_=in_, func=func, **kw), r=r, w=w)

    def tt(eng, out, in0, in1, op, r, w):
        return P.op(eng, lambda e: e.tensor_tensor(out=out, in0=in0, in1=in1, op=op), r=r, w=w)

    def ts(eng, out, in0, s1, s2, op0, op1, r, w):
        if op1 is None:
            return P.op(eng, lambda e: e.tensor_scalar(out=out, in0=in0, scalar1=s1, scalar2=None, op0=op0), r=r, w=w)
        return P.op(eng, lambda e: e.tensor_scalar(out=out, in0=in0, scalar1=s1, scalar2=s2, op0=op0, op1=op1), r=r, w=w)

    def stt(out, in0, scalar, in1, op0, op1, r, w):
        return P.op("dve", lambda e: e.scalar_tensor_tensor(out=out, in0=in0, scalar=scalar, in1=in1, op0=op0, op1=op1), r=r, w=w)

    def mm(out, lhsT, rhs, start, stop, r, w):
        return P.op("pe", lambda e: e.matmul(out, lhsT=lhsT, rhs=rhs, start=start, stop=stop), r=r, w=w)

    def tr(out, in_, idn, r, w):
        return P.op("pe", lambda e: e.transpose(out, in_, idn), r=r, w=w)

    def cp(eng, out, in_, r, w):
        if eng == "act":
            return P.op("act", lambda e: e.copy(out=out, in_=in_), r=r, w=w)
        return P.op(eng, lambda e: e.tensor_copy(out=out, in_=in_), r=r, w=w)

    def split_cast(dst, src, nsplit, sem):
        rows = src.shape[0]
        step = rows // nsplit
        o = None
        for i in range(nsplit):
            P.auto += 1
            o = dma("pool", dst[i * step:(i + 1) * step, :], src[i * step:(i + 1) * step, :], [], ["_c%d" % P.auto], sem + "*")
        P.res[sem] = [o, {}, []]

    gdma(ident_f[:], ident_d, "ident_f")
    gdma(cmask[:], cmask_d, "cmask")
    gdma(flag[:], flag_d, "flag")
    gdma(gpre[:], pre_mix_g.rearrange("(k p) -> p k", p=128), "gpre", slow=True)
    gdma(gpre2[:], pre_ffn_g.rearrange("(k p) -> p k", p=128), "gpre2", slow=True)
    gdma(gpost[:], post_mix_g.partition_broadcast(128), "gpost")
    gdma(lng[:], ln_g.rearrange("(k p) -> p k", p=128), "lng", slow=True)
    gdma(lnb[:], ln_b.rearrange("(k p) -> p k", p=128), "lnb", slow=True)
    for k in range(8):
        gdma(w31[:, k, :], conv_w[:, k * 128:(k + 1) * 128].rearrange("t p -> p t"), "w31", slow=True)
    for t in range(3):
        for hh in range(4):
            gdma(w3[:, hh * 22:(hh + 1) * 22, t],
                 ffn_cw[t, hh * 2816:(hh + 1) * 2816].rearrange("(c p) -> p c", p=128), "w3", slow=True)
    cp("dve", ident_b[:], ident_f[:], ["ident_f"], ["ident_b"])
    P.op("pool", lambda e: e.memset(ones_f[:], 1.0), w=["ones_f"])
    P.op("pool", lambda e: e.memset(ones_b[:], 1.0), w=["ones_b"])
    P.op("pool", lambda e: e.memset(carry[:], 0.0), w=["carry"])
    P.op("pool", lambda e: e.memset(dmy[:], 0.0), w=["dmy"])
    split_cast(win_b, w_in, 4, "w_in")
    STOP = os.environ.get("KSTOP", "")
    if STOP == "p0":
        P.emit()
        es.close()
        return nc

    lamr = sbp("lamr", [64, G])
    lami = sbp("lami", [64, G])
    dtt = sbp("dtt", [64, G])
    lv = sbp("lv", [64, 17])
    cv = sbp("cv", [64, C + 1])
    gdma(lamr[:], lam_re.rearrange("g p -> p g"), "lamr", slow=True)
    gdma(lami[:], lam_im.rearrange("g p -> p g"), "lami", slow=True)
    gdma(dtt[:], log_dt.partition_broadcast(64), "dtt")
    gdma(lv[:], lvals_d, "lv")
    gdma(cv[:], cvals_d, "cv")
    bre = sbp("bre", [64, G, 16])
    bim = sbp("bim", [64, G, 16])
    gdma(bre[:], b_re.rearrange("g p h -> p g h"), "bre")
    gdma(bim[:], b_im.rearrange("g p h -> p g h"), "bim")
    cnat = sbp("cnat", [128, 2, 8, 64])
    gdma(cnat[:, 0], c_re.rearrange("(k g) h p -> (g h) k p", g=8), "cnat")
    gdma(cnat[:, 1], c_im.rearrange("(k g) h p -> (g h) k p", g=8), "cnat")
    dcol = sbp("dcol", [128, G])
    for i in range(8):
        gdma(dcol[16 * i:16 * i + 16, :], ssm_d.rearrange("(g h) -> h g", h=16), "dcol", slow=True)
    cT = sbp("cT", [64, 2, G, 16])
    for comp in range(2):
        for k in range(8):
            pt, pk = mm_slot()
            tr(pt[0:64, 0:128], cnat[:, comp, k, :], ident_f[:], ["cnat", "ident_f"], [pk])
            cp("act", cT[:, comp, 8 * k:8 * k + 8, :], pt[0:64, 0:128].rearrange("p (g h) -> p g h", h=16), [pk], ["cT"])
    act_op(dtt[:], dtt[:], AF.Exp, ["dtt"], ["dtt"])
    lrd = sbp("lrd", [64, G])
    lid = sbp("lid", [64, G])
    tt("dve", lrd[:], lamr[:], dtt[:], ALU.mult, ["lamr", "dtt"], ["lrd"])
    tt("dve", lid[:], lami[:], dtt[:], ALU.mult, ["lami", "dtt"], ["lid"])
    TWO_PI = 2.0 * math.pi
    MAGIC = 12582912.0

    def sincos(dst_c, dst_s, mag, ang, shape, key):
        t1 = sbp("sc1_" + key, shape)
        t2 = sbp("sc2_" + key, shape)
        for which, dst in ((0, dst_s), (1, dst_c)):
            src = ang
            if which == 1:
                ts("dve", t2[:], ang, math.pi / 2, None, ALU.add, None, [key + "ang"], [key + "t2"])
                src = t2[:]
            rk = [key + "ang", key + "t2"]
            ts("dve", t1[:], src, 1.0 / TWO_PI, MAGIC, ALU.mult, ALU.add, rk, [key + "t1"])
            ts("dve", t1[:], t1[:], -MAGIC, None, ALU.add, None, [key + "t1"], [key + "t1"])
            stt(t1[:], t1[:], -TWO_PI, src, ALU.mult, ALU.add, rk + [key + "t1"], [key + "t1"])
            ts("dve", t1[:], t1[:], math.pi, -math.pi, ALU.min, ALU.max, [key + "t1"], [key + "t1"])
            act_op(t1[:], t1[:], AF.Sin, [key + "t1"], [key + "t1"])
            tt("dve", dst, t1[:], mag, ALU.mult, [key + "t1", key + "mag"], [key + "dst%d" % which])

    ang = sbp("ang", [64, G, 17])
    mag = sbp("mag", [64, G, 17])
    apr = sbp("apr", [64, G, 17])
    api = sbp("api", [64, G, 17])
    lvb = lv[:, :].unsqueeze(1).broadcast_to((64, G, 17))
    tt("dve", ang[:], lid[:, :].unsqueeze(2).broadcast_to((64, G, 17)), lvb, ALU.mult, ["lid", "lv"], ["Aang"])
    tt("dve", mag[:], lrd[:, :].unsqueeze(2).broadcast_to((64, G, 17)), lvb, ALU.mult, ["lrd", "lv"], ["Amag0"])
    act_op(mag[:], mag[:], AF.Exp, ["Amag0"], ["Amag"])
    sincos(apr[:], api[:], mag[:], ang[:], [64, G, 17], "A")
    tang = sbp("tang", [64, G, C + 1])
    tone = sbp("tone", [64, G, C + 1])
    tco = sbp("tco", [64, G, C + 1])
    tsi = sbp("tsi", [64, G, C + 1])
    cvb = cv[:, :].unsqueeze(1).broadcast_to((64, G, C + 1))
    stt(tang[:], lid[:, :].unsqueeze(2).broadcast_to((64, G, C + 1)), 8.0, cvb, ALU.mult, ALU.mult, ["lid", "cv"], ["Tang"])
    P.op("pool", lambda e: e.memset(tone[:], 1.0), w=["Tmag"])
    sincos(tco[:], tsi[:], tone[:], tang[:], [64, G, C + 1], "T")
    dma("sp", tab_d[0], tco[:], ["Tdst1"], ["tab_d0"], "t0")
    dma("sp", tab_d[1], tsi[:], ["Tdst0"], ["tab_d1"], "t1")
    rt = sbp("rt", [64, G, C + 1])
    ts("dve", rt[:], lrd[:, :].unsqueeze(2).broadcast_to((64, G, C + 1)), 8.0, None, ALU.mult, None, ["lrd"], ["rt"])
    act_op(rt[:], rt[:], AF.Exp, ["rt"], ["rt"])
    dma("sp", tab_d[2], rt[:], ["rt"], ["tab_d2"], "t2")
    for comp, dst in ((0, tcos), (1, tsin), (2, rtab)):
        for par in range(2):
            cc_ = 1 if comp == 2 else C + 1
            dma("sp", dst[64 * par:64 * par + 64, :, :],
                tab_d[comp].rearrange("p (gp par) c -> par p gp c", par=2)[par][:, :, 0:cc_], ["tab_d%d" % comp], ["tabs%d%d" % (comp, par)], "tr%d%d" % (comp, par), slow=(comp == 2))
    TABS = ["tabs%d%d" % (c_, p_) for c_ in range(3) for p_ in range(2)]
    am1 = sbp("am1", [64, G])
    den = sbp("den", [64, G])
    wre = sbp("wre", [64, G])
    wim = sbp("wim", [64, G])
    t64a = sbp("t64a", [64, G])
    t64b = sbp("t64b", [64, G])
    a1r = apr[:, :, 9]
    a1i = api[:, :, 9]
    KA = ["Adst0", "Adst1"]
    ts("dve", am1[:], a1r, -1.0, None, ALU.add, None, KA, ["am1"])
    tt("dve", den[:], lamr[:], lamr[:], ALU.mult, ["lamr"], ["den"])
    tt("dve", t64a[:], lami[:], lami[:], ALU.mult, ["lami"], ["t64a"])
    tt("dve", den[:], den[:], t64a[:], ALU.add, ["den", "t64a"], ["den"])
    P.op("dve", lambda e: e.reciprocal(out=den[:], in_=den[:]), r=["den"], w=["den"])
    tt("dve", wre[:], am1[:], lamr[:], ALU.mult, ["am1", "lamr"], ["wre"])
    tt("dve", t64a[:], a1i, lami[:], ALU.mult, KA + ["lami"], ["t64a"])
    tt("dve", wre[:], wre[:], t64a[:], ALU.add, ["wre", "t64a"], ["wre"])
    tt("dve", wre[:], wre[:], den[:], ALU.mult, ["wre", "den"], ["wre"])
    tt("dve", wim[:], a1i, lamr[:], ALU.mult, KA + ["lamr"], ["wim"])
    tt("dve", t64b[:], am1[:], lami[:], ALU.mult, ["am1", "lami"], ["t64b"])
    tt("dve", wim[:], wim[:], t64b[:], ALU.subtract, ["wim", "t64b"], ["wim"])
    tt("dve", wim[:], wim[:], den[:], ALU.mult, ["wim", "den"], ["wim"])
    bbr = sbp("bbr", [64, G, 16])
    bbi = sbp("bbi", [64, G, 16])
    t16 = sbp("t16", [64, G, 16])
    wreb = wre[:, :].unsqueeze(2).broadcast_to((64, G, 16))
    wimb = wim[:, :].unsqueeze(2).broadcast_to((64, G, 16))
    tt("dve", bbr[:], bre[:], wreb, ALU.mult, ["bre", "wre"], ["bbr"])
    tt("dve", t16[:], bim[:], wimb, ALU.mult, ["bim", "wim"], ["t16"])
    tt("dve", bbr[:], bbr[:], t16[:], ALU.subtract, ["bbr", "t16"], ["bbr"])
    tt("dve", bbi[:], bim[:], wreb, ALU.mult, ["bim", "wre"], ["bbi"])
    tt("dve", t16[:], bre[:], wimb, ALU.mult, ["bre", "wim"], ["t16"])
    tt("dve", bbi[:], bbi[:], t16[:], ALU.add, ["bbi", "t16"], ["bbi"])

    GB = 8
    Hre = sbp("Hre", [64, GB, 8, 16])
    Him = sbp("Him", [64, GB, 8, 16])
    Gre = sbp("Gre", [64, GB, 8, 16])
    Gim = sbp("Gim", [64, GB, 8, 16])
    Ere = sbp("Ere", [64, GB, 8, 16])
    Ein = sbp("Ein", [64, GB, 8, 16])
    tq1 = sbp("tq1", [64, GB, 8, 16])
    MOst = sbp("MOst", [64, GB, 2, 128], BF16)
    MIst = sbp("MIst", [128, GB, 128], BF16)
    MSst = sbp("MSst", [128, GB, 128], BF16)
    tmi = sbp("tmi", [128, 128])
    SH = (64, GB, 8, 16)

    def cmul(dre, dim_, pr, pi, vr, vi, kd, neg_im=False):
        tt("dve", dre, pr, vr, ALU.mult, KA + ["bbr", "bbi", "cT"], [kd + "r"])
        tt("dve", tq1[:], pi, vi, ALU.mult, KA + ["bbr", "bbi", "cT"], ["tq1"])
        tt("dve", dre, dre, tq1[:], ALU.subtract, [kd + "r", "tq1"], [kd + "r"])
        tt("dve", dim_, pr, vi, ALU.mult, KA + ["bbr", "bbi", "cT"], [kd + "i"])
        tt("dve", tq1[:], pi, vr, ALU.mult, KA + ["bbr", "bbi", "cT"], ["tq1"])
        if neg_im:
            stt(dim_, dim_, -1.0, tq1[:], ALU.mult, ALU.subtract, [kd + "i", "tq1"], [kd + "i"])
        else:
            tt("dve", dim_, dim_, tq1[:], ALU.add, [kd + "i", "tq1"], [kd + "i"])

    for bt in range(G // GB):
        g0 = bt * GB
        gs = slice(g0, g0 + GB)
        def pw(t, lo, hi, rev):
            a = t[:, gs, lo:hi]
            if rev:
                a = t[:, gs, hi - 1:lo - 1 if lo > 0 else None:-1] if False else a
            return a
        for i in range(8):
            for (dr, di, idx) in ((Hre, Him, 7 - i), (Gre, Gim, 15 - i)):
                pr = apr[:, gs, idx:idx + 1].broadcast_to((64, GB, 16))
                pi = api[:, gs, idx:idx + 1].broadcast_to((64, GB, 16))
                kd = "H" if dr is Hre else "Gm"
                o_r, o_i = dr[:, :, i, :], di[:, :, i, :]
                tq = tq1[:, :, i, :]
                tt("dve", o_r, pr, bbr[:, gs, :], ALU.mult, KA + ["bbr"], [kd + "r"])
                tt("dve", tq, pi, bbi[:, gs, :], ALU.mult, KA + ["bbi"], ["tq1"])
                tt("dve", o_r, o_r, tq, ALU.subtract, [kd + "r", "tq1"], [kd + "r"])
                tt("dve", o_i, pr, bbi[:, gs, :], ALU.mult, KA + ["bbi"], [kd + "i"])
                tt("dve", tq, pi, bbr[:, gs, :], ALU.mult, KA + ["bbr"], ["tq1"])
                tt("dve", o_i, o_i, tq, ALU.add, [kd + "i", "tq1"], [kd + "i"])
        pr = apr[:, gs, 9:17].unsqueeze(3).broadcast_to(SH)
        pi = api[:, gs, 9:17].unsqueeze(3).broadcast_to(SH)
        cr = cT[:, 0, gs, :].unsqueeze(2).broadcast_to(SH)
        ci = cT[:, 1, gs, :].unsqueeze(2).broadcast_to(SH)
        cmul(Ere[:], Ein[:], pr, pi, cr, ci, "E", neg_im=True)
        cp("act", MOst[:, :, 0, :], Ere[:].rearrange("p g j h -> p g (j h)"), ["Er"], ["MOst"])
        cp("act", MOst[:, :, 1, :], Ein[:].rearrange("p g j h -> p g (j h)"), ["Ei"], ["MOst"])
        for gl in range(GB):
            g = g0 + gl
            pt, pk = mm_slot()
            mm(pt[:, 0:128], Hre[:, gl].rearrange("p i h -> p (i h)"), Ere[:, gl].rearrange("p j h -> p (j h)"),
               True, False, ["Hr", "Er"], [pk])
            mm(pt[:, 0:128], Him[:, gl].rearrange("p i h -> p (i h)"), Ein[:, gl].rearrange("p j h -> p (j h)"),
               False, True, ["Hi", "Ei"], [pk])
            tt("dve", tmi[:], pt[:, 0:128], cmask[:], ALU.mult, [pk, "cmask"], ["tmi"])
            stt(MIst[:, gl, :], ident_f[:], dcol[:, g:g + 1], tmi[:], ALU.mult, ALU.add, ["ident_f", "dcol", "tmi"], ["MIst"])
            pt2, pk2 = mm_slot()
            tr(pt2[:, 0:64], Gre[:, gl].rearrange("p i h -> p (i h)"), ident_f[0:64, 0:64], ["Gmr", "ident_f"], [pk2])
            tr(pt2[:, 64:128], Gim[:, gl].rearrange("p i h -> p (i h)"), ident_f[0:64, 0:64], ["Gmi", "ident_f"], [pk2])
            cp("act", MSst[:, gl, :], pt2[:, 0:128], [pk2], ["MSst"])
        dma("sp", MI_d[gs].rearrange("g k m -> k g m"), MIst[:], ["MIst"], ["MI_d"], "m0")
        dma("sp", MS_d[gs].rearrange("g k m -> k g m"), MSst[:], ["MSst"], ["MS_d"], "m1")
        dma("sp", MO_d[gs].rearrange("g c p m -> p g c m"), MOst[:], ["MOst"], ["MO_d"], "m2")

    if STOP == "prep":
        P.emit()
        pes.close()
        es.close()
        return nc


    prep_dmas = [o for o in P.ops if o.dsem is not None and o.eng == "sp"]
    jk = []
    for en in ("pe", "act", "dve", "pool", "sp"):
        if en == "pe":
            o = P.op("pe", lambda e: e.matmul(mmps[0][0:8, 0:8], lhsT=ident_f[0:8, 0:8], rhs=ident_f[0:8, 0:8], start=True, stop=True), r=["ident_f"], w=["bar_pe", "mm0"])
        elif en == "act":
            o = P.op("act", lambda e: e.copy(out=dmy[:, 0:1], in_=ident_f[:, 0:1]), r=["ident_f", "dmy"], w=["bar_act"])
        elif en == "dve":
            o = P.op("dve", lambda e: e.tensor_copy(out=dmy[:, 1:2], in_=ident_f[:, 0:1]), r=["ident_f", "dmy"], w=["bar_dve"])
        elif en == "pool":
            o = P.op("pool", lambda e: e.tensor_copy(out=dmy[:, 2:3], in_=ident_f[:, 0:1]), r=["ident_f", "dmy"], w=["bar_pool"])
        else:
            o = P.op("sp", lambda e: e.dma_start(out=dmy_d[:, 0:1], in_=dmy[:, 4:5], allow_slow_non_contiguous=True), r=["dmy"], w=["bar_sp"], dma="bar")
            o.deps = list(o.deps) + [d for d in prep_dmas if not d.dsem.endswith("*") or True]
        jk.append(o)
    bars = ["bar_pe", "bar_act", "bar_dve", "bar_pool", "bar_sp"]
    P.op("pe", lambda e: e.matmul(mmps[0][0:8, 8:16], lhsT=ident_f[0:8, 0:8], rhs=ident_f[0:8, 0:8], start=True, stop=True), r=bars + ["ident_f"], w=["bar2_pe", "mm0"])
    P.op("act", lambda e: e.copy(out=dmy[:, 5:6], in_=ident_f[:, 0:1]), r=bars + ["ident_f"], w=["bar2_act"])
    P.op("dve", lambda e: e.tensor_copy(out=dmy[:, 6:7], in_=ident_f[:, 0:1]), r=bars + ["ident_f"], w=["bar2_dve"])
    P.op("pool", lambda e: e.tensor_copy(out=dmy[:, 7:8], in_=ident_f[:, 0:1]), r=bars + ["ident_f"], w=["bar2_pool"])
    P.op("sp", lambda e: e.dma_start(out=dmy_d[:, 1:2], in_=dmy[:, 4:5], allow_slow_non_contiguous=True), r=bars + ["dmy"], w=["bar2_sp"], dma="bar2")
    P.emit(final=False)
    pes.close()

    x_tm = sb("x_tm", [128, N // 128 if N >= 128 else 1, D])
    f_tm = sb("f_tm", [128, N // 128 if N >= 128 else 1, D])
    xn_tm = sb("xn_tm", [128, 1, D], BF16)
    stat = sb("stat", [128, 16])
    h_fm = sb("h_fm", [128, 16, N], BF16)
    h2_fm = sb("h2_fm", [128, 16, N + 2], BF16)
    act = sb("act", [128, 44, N], BF16)
    z_ext = sb("z_ext", [128, 8, 32 + N], BF16)
    sig = sb("sig", [128, 2, N])
    rr = sb("rr", [128, 4, N])
    sqsb = sb("sqsb", [128, 2, N])
    hl = sb("hl", [128, 4, 2, N], BF16)
    lnst = sb("lnst", [128, 4, N])
    diag = sb("diag", [128, 8, 128], BF16)
    Dre = sb("Dre", [128, 16, C])
    Dim = sb("Dim", [128, 16, C])
    Xre = sb("Xre", [128, 16, C + 1])
    Xim = sb("Xim", [128, 16, C + 1])
    Sf = sb("Sf", [128, 2, 16, C + 1])
    Sb = sb("Sb", [128, 2, 32, C], BF16)
    tmpA = sb("tmpA", [128, 16, C + 1])
    tmpB = sb("tmpB", [128, 16, C + 1])
    Ysb = sb("Ysb", [128, G, C])
    convsb = Ysb[:, :, :].rearrange("p g c -> p (g c)").rearrange("p (k n) -> p k n", n=N)
    MIs = sb("MIs", [128, 2, 8, 128], BF16)
    MSs = sb("MSs", [128, 2, 8, 128], BF16)
    MOs = sb("MOs", [128, 2, 4, 2, 128], BF16)
    WB = 256
    wring = sb("wring", [128, 3, 16, WB], BF16)
    DBW = 128
    dring = sb("dring", [128, 2, 44, DBW], BF16)

    cat = act[:, 0:16, :]
    u_perms = [act[:, 16:24, :], act[:, 8:16, :]]
    U_pks = [act[:, 24:32, :].rearrange("p k (g c) -> p (k g) c", c=C),
             act[:, 32:40, :].rearrange("p k (g c) -> p (k g) c", c=C)]
    gy = act[:, 32:40, :]
    y_fm = f_tm[:, 0, :].rearrange("p (k n) -> p k n", n=N) if N * 8 <= D else None


    P.op("pool", lambda e: e.memset(z_ext[:], 0.0), w=["z_ext"] + ["z%d" % k_ for k_ in range(8)])
    P.op("pool", lambda e: e.memset(h2_fm[:], 0.0), w=["h2_fm", "h2halo"] + ["h2_fm%d" % k_ for k_ in range(16)])
    wslot = [0]
    dslot = [0]
    mslot = [0]

    def wload(src_ap, key, kchunks):
        s = wslot[0] % 3
        wslot[0] += 1
        rk = "wr%d" % s
        dma("sp", wring[:, s, 0:kchunks, :], src_ap.rearrange("(k p) n -> p k n", p=128), [key], [rk], rk)
        return wring[:, s], rk

    def tile(tok0, NT, kind, phase="ALL", ub=0):
        UPK = ["U_pk%d_%d" % (ub, i_) for i_ in range(8)]
        u_perm, U_pk, U_scr = u_perms[ub], U_pks[ub], U_scrs[ub]
        ubase = 16 if ub == 0 else 8
        pbase = 24 if ub == 0 else 32
        CT = NT // 8
        beng = "pool" if kind == "prefix" else "sp"
        TB = [(o, min(128, NT - o)) for o in range(0, NT, 128)]
        if phase != "SSM":
            for bi, (o, sz) in enumerate(TB):
                dma("sp", x_tm[0:sz, bi, :], x_d[tok0 + o:tok0 + o + sz, :], [], ["x_tm%d" % bi], "xl%d" % bi)

            def norm_to_fm(src_tm, dst_fm, col0, gp, srckeys, kpre):
                for bi, (o, sz) in enumerate(TB):
                    act_op(xn_tm[0:sz, 0, :], src_tm[0:sz, bi, :], AF.Square, [srckeys[bi]], ["xn0", "xnst%d" % bi],
                           accum=stat[0:sz, bi:bi + 1])
                    ts("dve", stat[0:sz, 4 + bi:5 + bi], stat[0:sz, bi:bi + 1], 1.0 / D, EPS, ALU.mult, ALU.add,
                       ["xnst%d" % bi], ["st%d" % bi])
                    act_op(stat[0:sz, 4 + bi:5 + bi], stat[0:sz, 4 + bi:5 + bi], AF.Sqrt, ["st%d" % bi], ["st%d" % bi])
                    P.op("dve", lambda e, bi=bi, sz=sz: e.reciprocal(out=stat[0:sz, 8 + bi:9 + bi], in_=stat[0:sz, 4 + bi:5 + bi]),
                         r=["st%d" % bi], w=["rs%d" % bi])
                    act_op(f_tm[0:sz, bi, :], src_tm[0:sz, bi, :], AF.Identity, [srckeys[bi], "rs%d" % bi],
                           ["f_tm%d" % bi] + (["y_fm%d" % k_ for k_ in range(8)] if bi == 0 else []),
                           scale=stat[0:sz, 8 + bi:9 + bi])
                if STOP == "A0":
                    raise StopBuild()
                for kc in range(16):
                    pt, pk = mm_slot()
                    for bi, (o, sz) in enumerate(TB):
                        tr(pt[:, o:o + sz], f_tm[0:sz, bi, kc * 128:(kc + 1) * 128], ident_f[0:sz, 0:sz],
                           ["f_tm%d" % bi, "ident_f"], [pk])
                    ts("dve", dst_fm[:, kc, col0:col0 + NT], pt[:, 0:NT], gp[:, kc:kc + 1], None, ALU.mult, None,
                       [pk, "gpre", "gpre2"], [kpre + "%d" % kc])

            norm_to_fm(x_tm, h_fm, 0, gpre, ["x_tm%d" % bi for bi in range(len(TB))], "h_fm")
            hk = ["h_fm%d" % kc for kc in range(16)]
            if STOP == "A":
                raise StopBuild()

            def proj_chunk(wt, cofs, rhs_fm, ncol, rkeys, kch=16):
                pt, pk = mm_slot()
                for k in range(kch):
                    mm(pt[:, 0:ncol], wt[:, k, cofs:cofs + 128], rhs_fm(k), k == 0, k == kch - 1, rkeys(k), [pk])
                return pt, pk

            grp_uw, grp_ur = [], []
            for blk in range(4):
                wt, wk = wload(win_b[:, blk * WB:(blk + 1) * WB], "w_in", 16)
                for c2 in range(2):
                    m = blk * 2 + c2
                    pt, pk = proj_chunk(wt, c2 * 128, lambda k: h_fm[:, k, 0:NT], NT, lambda k: [wk, hk[k]])
                    cp("act", u_perm[:, m, 0:NT].rearrange("p (i c) -> p c i", i=8),
                       pt[:, 0:NT].rearrange("p (c i) -> p c i", i=8), [pk], ["act%d" % (ubase + m)])
                    grp_uw.append(dma(beng, U_scr[:, 8 * m:8 * m + 8, :, 0:CT].rearrange("i g h c -> (g h) i c"),
                        u_perm[:, m, 0:NT].rearrange("p (i c) -> p i c", i=8), ["act%d" % (ubase + m)], ["U_scr%d_%d" % (ub, m)], ("ub%d" if beng == "pool" else "us%d") % ub))
            for i in range(8):
                grp_ur.append(dma(beng, U_pk[16 * i:16 * i + 16, :, 0:CT], U_scr[i, :, :, 0:CT].rearrange("g h c -> h g c"),
                    ["U_scr%d_%d" % (ub, m_) for m_ in range(8)], ["U_pk%d_%d" % (ub, i)] + (["act%d" % c_ for c_ in range(pbase, pbase + 8)] if i == 0 else []), ("ur%d" if beng == "pool" else "uq%d") % ub))
            for o_ in grp_uw:
                o_.dval = grp_uw[-1].dval
            for o_ in grp_ur:
                o_.dval = grp_ur[-1].dval
            if STOP == "B":
                raise StopBuild()
            if kind != "prefix":
                for q in range(4):
                    wv, wvk = wload(win_b[:, 1024 + q * WB:1024 + (q + 1) * WB], "w_in", 16)
                    wg, wgk = wload(win_b[:, 2048 + q * WB:2048 + (q + 1) * WB], "w_in", 16)
                    for c2 in range(2):
                        kk = q * 2 + c2
                        pg, pgk = proj_chunk(wg, c2 * 128, lambda k: h_fm[:, k, 0:NT], NT, lambda k: [wgk, hk[k]])
                        act_op(sig[:, kk % 2, 0:NT], pg[:, 0:NT], AF.Sigmoid, [pgk], ["sig%d" % (kk % 2)])
                        pv, pvk = proj_chunk(wv, c2 * 128, lambda k: h_fm[:, k, 0:NT], NT, lambda k: [wvk, hk[k]])
                        tt("dve", z_ext[:, kk, 32:32 + NT], pv[:, 0:NT], sig[:, kk % 2, 0:NT], ALU.mult,
                           [pvk, "sig%d" % (kk % 2)], ["z%d" % kk])

        if phase == "AB":
            return
        for half in range(2):
            gb0 = half * 32
            for sub in range(4):
                s = mslot[0] % 2
                mslot[0] += 1
                gsub = slice(gb0 + sub * 8, gb0 + sub * 8 + 8)
                dma("sp", MSs[:, s], MS_d[gsub].rearrange("g k m -> k g m"), ["MS_d"], ["MSs%d" % s], "ms%d" % s)
                for gl in range(8):
                    g = gb0 + sub * 8 + gl
                    par, gpl = g % 2, (g - gb0) // 2
                    mm(Lre[64 * par:64 * par + 64, gpl * CT:(gpl + 1) * CT], MSs[:, s, gl, 0:64], U_pk[:, g, 0:CT],
                       True, True, ["MSs%d" % s, ] + UPK, ["Lre"])
                    mm(Lim[64 * par:64 * par + 64, gpl * CT:(gpl + 1) * CT], MSs[:, s, gl, 64:128], U_pk[:, g, 0:CT],
                       True, True, ["MSs%d" % s, ] + UPK, ["Lim"])
                if sub == 3 and STOP == "L":
                    raise StopBuild()
                if sub == 3:
                    gps = slice(half * 16, half * 16 + 16)
                    lre = Lre[:, 0:16 * CT].rearrange("p (g c) -> p g c", c=CT)
                    lim = Lim[:, 0:16 * CT].rearrange("p (g c) -> p g c", c=CT)
                    co = tcos[:, gps, 1:CT + 1]
                    si = tsin[:, gps, 1:CT + 1]
                    tA = tmpA[:, :, 0:CT]
                    tB = tmpB[:, :, 0:CT]
                    tt("dve", Dre[:, :, 0:CT], lre, co, ALU.mult, ["Lre", ] + TABS, ["Dre"])
                    tt("dve", tA, lim, si, ALU.mult, ["Lim", ] + TABS, ["tmpA"])
                    tt("dve", Dre[:, :, 0:CT], Dre[:, :, 0:CT], tA, ALU.add, ["Dre", "tmpA"], ["Dre"])
                    tt("dve", Dim[:, :, 0:CT], lim, co, ALU.mult, ["Lim", ] + TABS, ["Dim"])
                    tt("dve", tB, lre, si, ALU.mult, ["Lre", ] + TABS, ["tmpB"])
                    tt("dve", Dim[:, :, 0:CT], Dim[:, :, 0:CT], tB, ALU.subtract, ["Dim", "tmpB"], ["Dim"])
                    cp("dve", Xre[:, :, 0:1], carry[:, 0, gps].unsqueeze(2), ["carry"], ["Xre"])
                    cp("dve", Xim[:, :, 0:1], carry[:, 1, gps].unsqueeze(2), ["carry"], ["Xim"])
                    for gpl in range(16):
                        gp = half * 16 + gpl
                        for (X, Dd, kx, kd_) in ((Xre, Dre, "Xre", "Dre"), (Xim, Dim, "Xim", "Dim")):
                            P.op("dve", lambda e, X=X, Dd=Dd, gpl=gpl, gp=gp: e.tensor_tensor_scan(
                                out=X[:, gpl, 1:CT + 1], data0=rtab[:, gp, 0:1].broadcast_to((128, CT)), data1=Dd[:, gpl, 0:CT],
                                initial=X[:, gpl, 0:1], op0=ALU.mult, op1=ALU.add),
                                r=[kd_, *TABS, kx], w=[kx])
                    if STOP == "scan":
                        raise StopBuild()
                    co = tcos[:, gps, 0:CT + 1]
                    si = tsin[:, gps, 0:CT + 1]
                    tA = tmpA[:, :, 0:CT + 1]
                    tB = tmpB[:, :, 0:CT + 1]
                    sre = Sf[:, 0, :, 0:CT + 1]
                    sim = Sf[:, 1, :, 0:CT + 1]
                    xr = Xre[:, :, 0:CT + 1]
                    xi = Xim[:, :, 0:CT + 1]
                    seng = "dve" if kind == "prefix" else "pool"
                    tt(seng, sre, xr, co, ALU.mult, ["Xre", ] + TABS, ["Sre"])
                    tt(seng, tA, xi, si, ALU.mult, ["Xim", ] + TABS, ["tmpA"])
                    tt(seng, sre, sre, tA, ALU.subtract, ["Sre", "tmpA"], ["Sre"])
                    tt(seng, sim, xi, co, ALU.mult, ["Xim", ] + TABS, ["Sim"])
                    tt(seng, tB, xr, si, ALU.mult, ["Xre", ] + TABS, ["tmpB"])
                    tt(seng, sim, sim, tB, ALU.add, ["Sim", "tmpB"], ["Sim"])
                    cp("act", carry[:, 0, gps], Sf[:, 0, :, CT], ["Sre"], ["carry"])
                    cp("act", carry[:, 1, gps], Sf[:, 1, :, CT], ["Sim"], ["carry"])
                    if kind != "prefix":
                        cp("act", Sb[:, 0, gps, 0:CT], Sf[:, 0, :, 0:CT], ["Sre"], ["Sb"])
                        cp("act", Sb[:, 1, gps, 0:CT], Sf[:, 1, :, 0:CT], ["Sim"], ["Sb"])
            if kind == "prefix":
                continue
            for sub in range(4):
                s = mslot[0] % 2
                mslot[0] += 1
                gsub = slice(gb0 + sub * 8, gb0 + sub * 8 + 8)
                dma("sp", MIs[:, s], MI_d[gsub].rearrange("g k m -> k g m"), ["MI_d"], ["MIs%d" % s], "mi%d" % s)
                for par in range(2):
                    for comp in range(2):
                        dma("sp", MOs[64 * par:64 * par + 64, s, :, comp, :],
                            MO_d[gsub].rearrange("(gp par) c p m -> par c p gp m", par=2)[par, comp],
                            ["MO_d"], ["MOs%d_%d%d" % (s, par, comp)], "mo%d_%d%d" % (s, par, comp))
                for gl in range(8):
                    g = gb0 + sub * 8 + gl
                    par, gp = g % 2, g // 2
                    gi = g - gb0
                    yp = Yps[gi // 16]
                    yk = "Yps%d" % (gi // 16)
                    cs = slice((gi % 16) * CT, (gi % 16 + 1) * CT)
                    mm(yp[:, cs], MIs[:, s, gl, :], U_pk[:, g, 0:CT], True, False, ["MIs%d" % s, ] + UPK, [yk])
                    mm(yp[:, cs], MOs[64 * par:64 * par + 64, s, gl // 2, 0, :], Sb[64 * par:64 * par + 64, 0, gp, 0:CT],
                       False, False, ["MOs%d_%d%d" % (s, a_, b_) for a_ in range(2) for b_ in range(2)] + ["Sb"], [yk])
                    mm(yp[:, cs], MOs[64 * par:64 * par + 64, s, gl // 2, 1, :], Sb[64 * par:64 * par + 64, 1, gp, 0:CT],
                       False, True, ["MOs%d_%d%d" % (s, a_, b_) for a_ in range(2) for b_ in range(2)] + ["Sb"], [yk])
            for hh in range(2):
                cp("act", Ysb[:, gb0 + 16 * hh:gb0 + 16 * hh + 16, 0:CT],
                   Yps[hh][:, 0:16 * CT].rearrange("p (g c) -> p g c", c=CT), ["Yps%d" % hh], ["Ysb"] + ["cv%d" % k_ for k_ in range(8)])
        if kind == "prefix":
            return
        grp_yw = []
        for j in range(8):
            grp_yw.append(dma(beng, Y_scr[j, :, :, 0:CT].rearrange("g h c -> h g c"), Ysb[16 * j:16 * j + 16, :, 0:CT],
                ["Ysb"], ["Y_scr%d" % j], "yb"))
        for o_ in grp_yw:
            o_.dval = grp_yw[-1].dval
        for k in range(8):
            dma(beng, y_fm[:, k, 0:NT].rearrange("p (j c) -> p j c", j=8),
                Y_scr[:, 8 * k:8 * k + 8, :, 0:CT].rearrange("j g h c -> (g h) j c"),
                ["Y_scr%d" % j_ for j_ in range(8)] + ([] if k == 0 else ["f_tm0"]),
                ["y_fm%d" % k] + (["f_tm0"] if k == 0 else []), "yr%d" % k)

        if kind == "main" and STOP == "m_ssm":
            raise StopBuild()
        dgi = [0]
        sp_, spk = Lre, "Lre"
        sq_, sqk = Lim, "Lim"
        for kk in range(8):
            if kind == "main" and STOP == "c1b" and kk == 1:
                raise StopBuild()
            if kind == "main" and STOP == "c1c" and kk == 2:
                raise StopBuild()
            pt, pk = mm_slot()
            for t in range(KC):
                ds_ = dgi[0] % 8
                dgi[0] += 1
                if t % 2 == 0:
                    act_op(diag[:, ds_, :], ident_b[:], AF.Identity, ["ident_b", "w31"], ["dg%d" % ds_],
                           scale=w31[:, kk, t:t + 1])
                else:
                    ts("dve", diag[:, ds_, :], ident_b[:], w31[:, kk, t:t + 1], None, ALU.mult, None,
                       ["ident_b", "w31"], ["dg%d" % ds_])
                tofs = 2 + t - ((t % 2) if os.environ.get("KEVEN") else 0)
                mm(pt[:, 0:NT], diag[:, ds_, :], z_ext[:, kk, tofs:tofs + NT], t == 0, t == KC - 1,
                   ["dg%d" % ds_, "z%d" % kk, "z_ext"], [pk])
            if kind == "main" and STOP == "c1":
                raise StopBuild()
            cp("act", convsb[:, kk, 0:NT], pt[:, 0:NT], [pk], ["cv%d" % kk, "Ysb"])
            if kind == "main" and STOP == "c1d":
                raise StopBuild()
            b2 = kk % 2
            act_op(sqsb[:, b2, 0:NT], pt[:, 0:NT], AF.Square, [pk], ["sq%d" % b2])
            cp("act", hl[:, 0, b2, 0:NT], pt[:, 0:NT], [pk], ["chi%d" % b2])
            tt("dve", hl[:, 1, b2, 0:NT], convsb[:, kk, 0:NT], hl[:, 0, b2, 0:NT], ALU.subtract,
               ["cv%d" % kk, "chi%d" % b2], ["clo%d" % b2])
            cp("act", hl[:, 2, b2, 0:NT], sqsb[:, b2, 0:NT], ["sq%d" % b2], ["shi%d" % b2])
            tt("dve", hl[:, 3, b2, 0:NT], sqsb[:, b2, 0:NT], hl[:, 2, b2, 0:NT], ALU.subtract,
               ["sq%d" % b2, "shi%d" % b2], ["slo%d" % b2])
            if kind == "main" and STOP == "c1f":
                raise StopBuild()
            mm(sp_[:, 0:NT], ones_b[:], hl[:, 0, b2, 0:NT], kk == 0, False, ["ones_b", "chi%d" % b2], [spk])
            mm(sp_[:, 0:NT], ones_b[:], hl[:, 1, b2, 0:NT], False, kk == 7, ["ones_b", "clo%d" % b2], [spk])
            mm(sq_[:, 0:NT], ones_b[:], hl[:, 2, b2, 0:NT], kk == 0, False, ["ones_b", "shi%d" % b2], [sqk])
            mm(sq_[:, 0:NT], ones_b[:], hl[:, 3, b2, 0:NT], False, kk == 7, ["ones_b", "slo%d" % b2], [sqk])
        if kind == "main" and STOP == "c2":
            raise StopBuild()
        mean = lnst[:, 0, 0:NT]
        var = lnst[:, 1, 0:NT]
        rstd = lnst[:, 2, 0:NT]
        act_op(mean, sp_[:, 0:NT], AF.Copy, [spk], ["ln_m"], scale=1.0 / 1024)
        tt("dve", var, mean, mean, ALU.mult, ["ln_m"], ["ln_v"])
        stt(var, sq_[:, 0:NT], 1.0 / 1024, var, ALU.mult, ALU.subtract, [sqk, "ln_v"], ["ln_v"])
        ts("dve", var, var, EPS, None, ALU.add, None, ["ln_v"], ["ln_v"])
        act_op(var, var, AF.Sqrt, ["ln_v"], ["ln_v"])
        P.op("dve", lambda e: e.reciprocal(out=rstd, in_=var), r=["ln_v"], w=["ln_r"])
        for kk in range(8):
            tt("dve", convsb[:, kk, 0:NT], convsb[:, kk, 0:NT], mean, ALU.subtract, ["cv%d" % kk, "ln_m"], ["cv%d" % kk])
            tt("dve", convsb[:, kk, 0:NT], convsb[:, kk, 0:NT], rstd, ALU.mult, ["cv%d" % kk, "ln_r"], ["cv%d" % kk])
            act_op(cat[:, 8 + kk, 0:NT], convsb[:, kk, 0:NT], AF.Silu, ["cv%d" % kk, "lng", "lnb"], ["act%d" % (8 + kk)],
                   scale=lng[:, kk:kk + 1], bias=lnb[:, kk:kk + 1])
        cp("pool", z_ext[:, :, 0:32], z_ext[:, :, NT:NT + 32], ["z%d" % kk for kk in range(8)] + ["z_ext"],
           ["z_ext"] + ["z%d" % kk for kk in range(8)])

        if kind == "main" and STOP == "m_conv":
            raise StopBuild()
        for k in range(8):
            act_op(gy[:, k, 0:NT], y_fm[:, k, 0:NT], AF.Gelu_apprx_tanh, ["y_fm%d" % k], ["act%d" % (32 + k)])
        for blk in range(4):
            wt, wk = wload(wglu_b[:, blk * WB:(blk + 1) * WB], "w_glu", 8)
            for c2 in range(2):
                m = blk * 2 + c2
                pt, pk = proj_chunk(wt, c2 * 128, lambda k: gy[:, k, 0:NT], NT, lambda k: [wk, "act%d" % (32 + k)], kch=8)
                act_op(sig[:, m % 2, 0:NT], pt[:, 0:NT], AF.Sigmoid, [pk], ["sig%d" % (m % 2)])
                act_op(rr[:, m % 2, 0:NT], y_fm[:, m, 0:NT], AF.Gelu_apprx_tanh, ["y_fm%d" % m], ["rr%d" % (m % 2)])
                tt("dve", cat[:, m, 0:NT].rearrange("p (c j) -> p j c", j=8),
                   rr[:, m % 2, 0:NT].rearrange("p (j c) -> p j c", j=8),
                   sig[:, m % 2, 0:NT].rearrange("p (j c) -> p j c", j=8), ALU.mult,
                   ["rr%d" % (m % 2), "sig%d" % (m % 2)], ["act%d" % m])

        if kind == "main" and STOP == "m_glu":
            raise StopBuild()
        for nb in range(D // WB):
            wt, wk = wload(wout_b[:, nb * WB:(nb + 1) * WB], "w_out", 16)
            for bi, (o, sz) in enumerate(TB):
                pt, pk = mm_slot()
                for k in range(16):
                    mm(pt[0:sz, 0:WB], cat[:, k, o:o + sz], wt[:, k, :], k == 0, k == 15, ["act%d" % k, wk], [pk])
                cp("act", f_tm[0:sz, bi, nb * WB:(nb + 1) * WB], pt[0:sz, 0:WB], [pk], ["f_tm%d" % bi] + (["y_fm%d" % k_ for k_ in range(8)] if bi == 0 else []))

        def post_norm_add(gtile, gkey):
            for bi, (o, sz) in enumerate(TB):
                act_op(xn_tm[0:sz, 0, :], f_tm[0:sz, bi, :], AF.Square, ["f_tm%d" % bi], ["xn0", "xnst%d" % bi],
                       accum=stat[0:sz, bi:bi + 1])
                ts("dve", stat[0:sz, 4 + bi:5 + bi], stat[0:sz, bi:bi + 1], 1.0 / D, EPS, ALU.mult, ALU.add,
                   ["xnst%d" % bi], ["st%d" % bi])
                act_op(stat[0:sz, 4 + bi:5 + bi], stat[0:sz, 4 + bi:5 + bi], AF.Sqrt, ["st%d" % bi], ["st%d" % bi])
                P.op("dve", lambda e, bi=bi, sz=sz: e.reciprocal(out=stat[0:sz, 8 + bi:9 + bi], in_=stat[0:sz, 4 + bi:5 + bi]),
                     r=["st%d" % bi], w=["rs%d" % bi])
                stt(f_tm[0:sz, bi, :], f_tm[0:sz, bi, :], stat[0:sz, 8 + bi:9 + bi], gtile[0:sz, :], ALU.mult, ALU.mult,
                    ["f_tm%d" % bi, "rs%d" % bi, gkey], ["f_tm%d" % bi])
                tt("dve", x_tm[0:sz, bi, :], x_tm[0:sz, bi, :], f_tm[0:sz, bi, :], ALU.add,
                   ["x_tm%d" % bi, "f_tm%d" % bi], ["x_tm%d" % bi])

        post_norm_add(gpost, "gpost")
        if kind == "main":
            dma("sp", gpost[:], post_ffn_g.partition_broadcast(128), [], ["gpost"], "gp")
        if kind == "main":
            pass
        norm_to_fm(x_tm, h2_fm, 2, gpre2, ["x_tm%d" % bi for bi in range(len(TB))], "h2_fm")
        h2k = ["h2_fm%d" % kc for kc in range(16)]
        if kind == "mini":
            ts("dve", h2_fm[:, :, 0:2], h2_fm[:, :, NT:NT + 2], flag[:, 0:1], None, ALU.mult, None,
               h2k + ["flag"], ["h2_fm", "h2halo"])
            return
        if kind == "main" and STOP == "m_wout":
            raise StopBuild()
        for q in range(22):
            wg, wgk = wload(wup_b[:, q * WB:(q + 1) * WB], "w_up", 16)
            wv, wvk = wload(wup_b[:, DFF + q * WB:DFF + (q + 1) * WB], "w_up", 16)
            for c2 in range(2):
                ch = q * 2 + c2
                res = []
                for (wt, wk, cidx, slot) in ((wg, wgk, ch, 0), (wv, wvk, 44 + ch, 1)):
                    pt, pk = proj_chunk(wt, c2 * 128, lambda k: h2_fm[:, k, 0:NT + 2], NT + 2,
                                        lambda k: [wk, h2k[k], "h2halo", "h2_fm"])
                    r_ = rr[:, 2 * (ch % 2) + slot, 0:NT]
                    rk_ = "rr%d" % (2 * (ch % 2) + slot)
                    act_op(r_, pt[:, 0:NT], AF.Identity, [pk, "w3"], [rk_], scale=w3[:, cidx, 0:1])
                    stt(r_, pt[:, 1:NT + 1], w3[:, cidx, 1:2], r_, ALU.mult, ALU.add, [pk, "w3", rk_], [rk_])
                    stt(r_, pt[:, 2:NT + 2], w3[:, cidx, 2:3], r_, ALU.mult, ALU.add, [pk, "w3", rk_], [rk_])
                    res.append((r_, rk_))
                (rg, rgk), (rv, rvk) = res
                act_op(rg, rg, AF.Gelu_apprx_tanh, [rgk], [rgk])
                tt("dve", act[:, ch, 0:NT], rg, rv, ALU.mult, [rgk, rvk], ["act%d" % ch])
        cp("pool", h2_fm[:, :, 0:2], h2_fm[:, :, NT:NT + 2], h2k + ["h2_fm"], ["h2_fm", "h2halo"] + h2k)
        if kind == "main" and STOP == "m_up":
            raise StopBuild()
        for nb in range(D // DBW):
            s = dslot[0] % 2
            dslot[0] += 1
            dk = "dr%d" % s
            dma("sp", dring[:, s], wdn_b[:, nb * DBW:(nb + 1) * DBW].rearrange("(k p) n -> p k n", p=128),
                ["w_down"], [dk], dk)
            for bi, (o, sz) in enumerate(TB):
                pt, pk = mm_slot()
                for k in range(44):
                    mm(pt[0:sz, 0:DBW], act[:, k, o:o + sz], dring[:, s, k, :], k == 0, k == 43,
                       ["act%d" % k, dk], [pk])
                cp("act", f_tm[0:sz, bi, nb * DBW:(nb + 1) * DBW], pt[0:sz, 0:DBW], [pk], ["f_tm%d" % bi])
        post_norm_add(gpost, "gpost")
        dma("sp", gpost[:], post_mix_g.partition_broadcast(128), [], ["gpost"], "gp")
        orow = tok0 - (NPRE * N + NMINI)
        for bi, (o, sz) in enumerate(TB):
            dma("sp", out_d[orow + o:orow + o + sz, :], x_tm[0:sz, bi, :], ["x_tm%d" % bi], ["out"], "os")

    split_cast(wglu_b, w_glu, 2, "w_glu")
    split_cast(wout_b, w_out, 4, "w_out")
    t0 = 0
    try:
        if STOP != "bar" and NPRE > 0:
            tile(0, N, "prefix", "AB", 0)
        for i in range(NPRE):
            if STOP == "bar":
                break
            if i + 1 < NPRE:
                tile((i + 1) * N, N, "prefix", "AB", (i + 1) % 2)
            tile(i * N, N, "prefix", "SSM", i % 2)
            t0 += N
    except StopBuild:
        P.emit()
        es.close()
        return nc
    split_cast(wup_b, w_up, 8, "w_up")
    split_cast(wdn_b, w_down, 8, "w_down")
    if STOP in ("bar", "prefix"):
        P.emit()
        es.close()
        return nc
    tile(t0, NMINI, "mini")
    if STOP == "mini":
        P.emit()
        es.close()
        return nc
    t0 += NMINI
    try:
        for i in range(NMAIN):
            tile(t0, N, "main")
            t0 += N
    except StopBuild:
        P.emit()
        es.close()
        return nc
    P.emit()
    es.close()
    return nc


def host_consts(C):
    ident = np.eye(128, dtype=np.float32)
    idx = np.arange(128) // 16
    cmask = (idx[:, None] <= idx[None, :]).astype(np.float32)
    lvals = np.tile(np.arange(-8, 9, dtype=np.float32)[None, :], (64, 1))
    cvals = np.tile(np.arange(0, C + 1, dtype=np.float32)[None, :], (64, 1))
    return ident, cmask, lvals, cvals


def run(inputs, NPRE, NMAIN, N=256, NMINI=32, trace=False):
    x = np.asarray(inputs["x"], dtype=np.float32)
    B, S, _ = x.shape
    seg = NMAIN * N
    nseg = S // seg
    assert B * nseg == NCORE
    nc = build(NPRE, NMAIN, N, NMINI)
    ident, cmask, lvals, cvals = host_consts(N // 8)
    shared = {"ident": ident, "cmask": cmask, "lvals": lvals, "cvals": cvals}
    for k, v in inputs.items():
        if k == "x":
            continue
        a = np.asarray(v, dtype=np.float32)
        shared[k] = np.ascontiguousarray(a.reshape(a.shape[1:]))
    pre = NPRE * N + NMINI
    in_maps = []
    for c in range(NCORE):
        b, sg = divmod(c, nseg)
        s0 = sg * seg
        xs = np.zeros((pre + seg, D), np.float32)
        lo = s0 - pre
        src_lo = max(lo, 0)
        xs[src_lo - lo:, :] = x[b, src_lo:s0 + seg, :]
        m = dict(shared)
        m["x"] = xs
        m["flag"] = np.full((128, 1), 1.0 if s0 > 0 else 0.0, np.float32)
        in_maps.append(m)
    res = run_bass_kernel_spmd(nc, in_maps, core_ids=list(range(NCORE)), trace=trace)
    out = np.zeros((B, S, D), np.float32)
    for c in range(NCORE):
        b, sg = divmod(c, nseg)
        out[b, sg * seg:(sg + 1) * seg, :] = res.results[c]["out"]
    return out, res


def kernel(**inputs):
    out, _ = run(inputs, NPRE=24, NMAIN=8)
    return out
```
